# Optimizing a Trainium2 kernel written in Bass

```python
import jax, jax.numpy as jnp
from jax import lax
import numpy as np

D_MODEL = 1024
BATCH = 8
SEQ = 8192
DEPTH = 2

N_MEM = 256
D_MIX = D_MODEL
POOL_WIDTH = D_MIX // 2
POOL_WINDOWS = (2, 4, 8, 16)
POOL_GROUPS = len(POOL_WINDOWS)
POOL_GROUP_DIM = POOL_WIDTH // POOL_GROUPS
SGU_WIDTH = D_MIX - POOL_WIDTH
SGU_HEADS = 4
SGU_HEAD_DIM = SGU_WIDTH // SGU_HEADS
CHUNK = 128
D_IN_PROJ = POOL_WIDTH + 2 * SGU_WIDTH
XATTN_HEADS = 4
XATTN_HEAD_DIM = D_MODEL // XATTN_HEADS
D_FF = 2816
CONV_WIDTH = 3
EPS = 1e-6

kernel_name = "hybrid_pool_sgu_memxattn_convffn"


def rmsnorm(x, g):
    xf = x.astype(jnp.float32)
    y = xf * lax.rsqrt(jnp.mean(xf * xf, axis=-1, keepdims=True) + EPS)
    return (y * g.astype(jnp.float32)).astype(x.dtype)


def layernorm_nobias(x, g):
    xf = x.astype(jnp.float32)
    mu = jnp.mean(xf, axis=-1, keepdims=True)
    xc = xf - mu
    y = xc * lax.rsqrt(jnp.mean(xc * xc, axis=-1, keepdims=True) + EPS)
    return (y * g.astype(jnp.float32)).astype(x.dtype)


def pool_mixer(p, pool_w, pool_scale):
    B, S, _ = p.shape
    pf = p.astype(jnp.float32)
    c = jnp.pad(jnp.cumsum(pf, axis=1), ((0, 0), (1, 0), (0, 0)))
    t = jnp.arange(S)
    diffs = []
    for gi, win in enumerate(POOL_WINDOWS):
        sl = slice(gi * POOL_GROUP_DIM, (gi + 1) * POOL_GROUP_DIM)
        cg = c[..., sl]
        prev = jnp.pad(cg, ((0, 0), (win - 1, 0), (0, 0)))[:, :S]
        count = jnp.minimum(t + 1, win).astype(jnp.float32)[None, :, None]
        diffs.append((cg[:, 1:] - prev) / count - pf[..., sl])
    d = jnp.stack(diffs, axis=2).astype(p.dtype)
    y = jnp.einsum('bsgc,gcd->bsgd', d, pool_w).reshape(B, S, POOL_WIDTH)
    return y * pool_scale


def sgu_mixer(u, v, sgu_g, sgu_w, sgu_b):
    B, S, _ = u.shape
    vn = layernorm_nobias(v, sgu_g)
    vc = vn.reshape(B, S // CHUNK, CHUNK, SGU_HEADS, SGU_HEAD_DIM)
    mask = jnp.tril(jnp.ones((CHUNK, CHUNK), dtype=bool))
    w_masked = jnp.where(mask[None], sgu_w, jnp.zeros_like(sgu_w))
    z = jnp.einsum('hts,bnshd->bnthd', w_masked, vc) + sgu_b.T[:, :, None]
    return u * z.reshape(B, S, SGU_WIDTH)


def mem_cross_attention(xn, mem, mem_g, wq, wk, wv, wo):
    B, S, _ = xn.shape
    memn = rmsnorm(mem, mem_g)
    q = (xn @ wq).reshape(B, S, XATTN_HEADS, XATTN_HEAD_DIM)
    k = (memn @ wk).reshape(B, N_MEM, XATTN_HEADS, XATTN_HEAD_DIM)
    v = (memn @ wv).reshape(B, N_MEM, XATTN_HEADS, XATTN_HEAD_DIM)
    s = jnp.einsum('bshd,bmhd->bhsm', q, k).astype(jnp.float32) * (XATTN_HEAD_DIM ** -0.5)
    pr = jax.nn.softmax(s, axis=-1).astype(v.dtype)
    o = jnp.einsum('bhsm,bmhd->bshd', pr, v).reshape(B, S, D_MODEL)
    return o @ wo


def conv_ffn(xn, w_up, conv_w, conv_b, w_down):
    S = xn.shape[1]
    h = xn @ w_up
    hp = jnp.pad(h, ((0, 0), (CONV_WIDTH - 1, 0), (0, 0)))
    hc = conv_b + sum(conv_w[k] * hp[:, k:k + S] for k in range(CONV_WIDTH))
    gate, val = jnp.split(hc, 2, axis=-1)
    return (jax.nn.silu(gate) * val) @ w_down


def setup_inputs(seed: int = 0) -> dict:
    key = jax.random.key(seed)
    ks = jax.random.split(key, 24)
    f32 = jnp.float32
    n = lambda k, shape, s: (jax.random.normal(k, shape, f32) * s)
    gain = lambda k, shape: 1.0 + 0.05 * jax.random.normal(k, shape, f32)
    L = DEPTH
    return {
        "x": jax.random.normal(ks[0], (BATCH, SEQ, D_MODEL), f32),
        "mem": jax.random.normal(ks[1], (BATCH, N_MEM, D_MODEL), f32),
        "norm_mix_g": gain(ks[2], (L, D_MODEL)),
        "w_in": n(ks[3], (L, D_MODEL, D_IN_PROJ), D_MODEL ** -0.5),
        "pool_w": n(ks[4], (L, POOL_GROUPS, POOL_GROUP_DIM, POOL_GROUP_DIM), POOL_GROUP_DIM ** -0.5),
        "pool_scale": 1.0 + 0.1 * jax.random.normal(ks[5], (L, POOL_WIDTH), f32),
        "sgu_g": gain(ks[6], (L, SGU_WIDTH)),
        "sgu_w": n(ks[7], (L, SGU_HEADS, CHUNK, CHUNK), CHUNK ** -0.5),
        "sgu_b": 1.0 + 0.05 * jax.random.normal(ks[8], (L, SGU_HEADS, CHUNK), f32),
        "w_out": n(ks[9], (L, D_MIX, D_MODEL), D_MIX ** -0.5),
        "norm_xattn_g": gain(ks[10], (L, D_MODEL)),
        "mem_norm_g": gain(ks[11], (L, D_MODEL)),
        "wq": n(ks[12], (L, D_MODEL, D_MODEL), D_MODEL ** -0.5),
        "wk": n(ks[13], (L, D_MODEL, D_MODEL), D_MODEL ** -0.5),
        "wv": n(ks[14], (L, D_MODEL, D_MODEL), D_MODEL ** -0.5),
        "wo": n(ks[15], (L, D_MODEL, D_MODEL), D_MODEL ** -0.5),
        "norm_ffn_g": gain(ks[16], (L, D_MODEL)),
        "w_up": n(ks[17], (L, D_MODEL, 2 * D_FF), D_MODEL ** -0.5),
        "conv_w": n(ks[18], (L, CONV_WIDTH, 2 * D_FF), CONV_WIDTH ** -0.5),
        "conv_b": n(ks[19], (L, 2 * D_FF), 0.02),
        "w_down": n(ks[20], (L, D_FF, D_MODEL), D_FF ** -0.5),
        "final_norm_g": gain(ks[21], (D_MODEL,)),
    }


def reference(x, mem, norm_mix_g, w_in, pool_w, pool_scale, sgu_g, sgu_w, sgu_b, w_out,
              norm_xattn_g, mem_norm_g, wq, wk, wv, wo,
              norm_ffn_g, w_up, conv_w, conv_b, w_down, final_norm_g):
    h = x
    for l in range(DEPTH):
        xn = rmsnorm(h, norm_mix_g[l])
        proj = xn @ w_in[l]
        p = proj[..., :POOL_WIDTH]
        uv = jax.nn.gelu(proj[..., POOL_WIDTH:], approximate=False)
        u, v = uv[..., :SGU_WIDTH], uv[..., SGU_WIDTH:]
        y_pool = pool_mixer(p, pool_w[l], pool_scale[l])
        y_sgu = sgu_mixer(u, v, sgu_g[l], sgu_w[l], sgu_b[l])
        h = h + jnp.concatenate([y_pool, y_sgu], axis=-1) @ w_out[l]
        xn = rmsnorm(h, norm_xattn_g[l])
        h = h + mem_cross_attention(xn, mem, mem_norm_g[l], wq[l], wk[l], wv[l], wo[l])
        xn = rmsnorm(h, norm_ffn_g[l])
        h = h + conv_ffn(xn, w_up[l], conv_w[l], conv_b[l], w_down[l])
    return rmsnorm(h, final_norm_g)
```

```python
import contextlib
import numpy as np
import concourse.bass as bass
import concourse.mybir as mybir
from concourse.bass_utils import run_bass_kernel_spmd

F32 = mybir.dt.float32
BF16 = mybir.dt.bfloat16
AF = mybir.ActivationFunctionType
ALU = mybir.AluOpType

D = 1024
KC = 8
NMEM = 256
DFF = 2816
NFC = 44
SUB = 512
NS = 2
T = SUB * NS
EPS = 1e-6
WINS = (2, 4, 8, 16)
NSLOT = 4
CONV_AHEAD = 6
NCS = 12
PIECE = 4096

LC = 728
O_GMIX, O_GX, O_GFFN, O_GMEM, O_PSC, O_SGG, O_CW0, O_CW1, O_CW2, O_CB, O_BBC = 0, 8, 16, 24, 32, 36, 40, 84, 128, 172, 216


class Eng:
    def __init__(self, name):
        self.name = name
        self.sem = None
        self.cnt = 0
        self.known = {}
        self.ops = []


class Buf:
    registry = {}

    def __init__(self, name, arena=None, lo=0, hi=0):
        self.name = name
        self.w = None
        self.r = {}
        self.arena = arena
        self.lo, self.hi = lo, hi
        self.al = []
        if arena is not None:
            lst = Buf.registry.setdefault(arena, [])
            for o in lst:
                if o.lo < hi and lo < o.hi:
                    o.al.append(self)
                    self.al.append(o)
            lst.append(self)


def _need(E, waits, ev, same_ok):
    if ev is None:
        return
    sem, val, owner = ev
    if same_ok and owner is E and E.name == "pe":
        return
    k = id(sem)
    if E.known.get(k, 0) >= val:
        return
    if k not in waits or waits[k][1] < val:
        waits[k] = (sem, val)


def _collect(E, reads, writes):
    waits = {}
    for b in reads:
        _need(E, waits, b.w, False)
    for b in writes:
        for x in [b] + b.al:
            _need(E, waits, x.w, True)
            for ev in x.r.values():
                _need(E, waits, ev, True)
    for k, (sem, val) in waits.items():
        E.known[k] = val
        E.ops.append(lambda e, s=sem, v=val: e.wait_ge(s, v))


STAGE = ["init"]
PE_LOG = []


def emit(E, fn, reads=(), writes=(), nmm=0):
    if nmm:
        PE_LOG.append((STAGE[0], nmm))
    _collect(E, reads, writes)
    E.cnt += 1
    ev = (E.sem, E.cnt, E)
    sem = E.sem
    E.ops.append(lambda e: fn(e).then_inc(sem, 1))
    for b in reads:
        b.r[id(sem)] = ev
    for b in writes:
        b.w = ev
        b.r = {}
    return ev


class DmaSem:
    def __init__(self, sem):
        self.sem = sem
        self.val = 0


def emit_dma(Q, fns, dsem, reads=(), writes=()):
    _collect(Q, reads, writes)
    dsem.val += 16 * len(fns)
    ev = (dsem.sem, dsem.val, None)
    s = dsem.sem
    for fn in fns:
        Q.ops.append(lambda e, fn=fn: fn(e).then_inc(s, 16))
    for b in reads:
        b.r[id(s)] = ev
    for b in writes:
        b.w = ev
        b.r = {}
    return ev


def build(S, L):
    NT = S // T
    del PE_LOG[:]
    STAGE[0] = "init"
    Buf.registry = {}
    nc = bass.Bass("TRN2", target_bir_lowering=False)
    dt_in = lambda name, shape: nc.dram_tensor(name, list(shape), F32, kind="ExternalInput").ap()
    x_d = dt_in("x", (S, D))
    mem_d = dt_in("mem", (NMEM, D))
    w_in_d = dt_in("w_in", (L, D, 1536))
    pool_w_d = dt_in("pool_w", (L, 4, 128, 128))
    sgu_w_d = dt_in("sgu_w", (L, 4, 128, 128))
    w_out_d = dt_in("w_out", (L, D, D))
    wq_d = dt_in("wq", (L, D, D))
    wk_d = dt_in("wk", (L, D, D))
    wv_d = dt_in("wv", (L, D, D))
    wo_d = dt_in("wo", (L, D, D))
    w_up_d = dt_in("w_up", (L, D, 2 * DFF))
    w_down_d = dt_in("w_down", (L, DFF, D))
    NC_ = L * LC + 8 + 64 + 1
    O_GFIN = L * LC
    O_INV = L * LC + 8
    O_EPS = L * LC + 72
    consts_d = dt_in("consts", (128, NC_))
    ident_d = dt_in("ident", (128, 128))
    mask_d = dt_in("trilmask", (128, 128))
    out_d = nc.dram_tensor("out", [S, D], F32, kind="ExternalOutput").ap()

    pieces = []

    def colpiece(W, c0):
        return [(lambda dst: dst.rearrange("p (k c) -> p k c", k=8),
                 W[:, c0:c0 + 512].rearrange("(k p) c -> p k c", p=128))]

    pidx = {}
    for l in range(L):
        for nm, W in (("wk", wk_d), ("wv", wv_d)):
            for pc in range(2):
                pidx[(nm, l, pc)] = len(pieces)
                pieces.append(colpiece(W[l], pc * 512))
    for l in range(L):
        for pc in range(3):
            pidx[("w_in", l, pc)] = len(pieces)
            pieces.append(colpiece(w_in_d[l], pc * 512))
        for nm, W in (("w_out", w_out_d), ("wq", wq_d), ("wo", wo_d)):
            for pc in range(2):
                pidx[(nm, l, pc)] = len(pieces)
                pieces.append(colpiece(W[l], pc * 512))
        for i in range(11):
            pidx[("w_up", l, i)] = len(pieces)
            pieces.append([
                (lambda dst: dst.rearrange("p (k c) -> p k c", k=8)[:, :, 0:256],
                 w_up_d[l][:, 256 * i:256 * i + 256].rearrange("(k p) c -> p k c", p=128)),
                (lambda dst: dst.rearrange("p (k c) -> p k c", k=8)[:, :, 256:512],
                 w_up_d[l][:, DFF + 256 * i:DFF + 256 * i + 256].rearrange("(k p) c -> p k c", p=128)),
            ])
        for m in range(8):
            pidx[("w_down", l, m)] = len(pieces)
            pieces.append([(lambda dst: dst[:, 0:2816].rearrange("p (k c) -> p k c", k=22),
                            w_down_d[l][:, m * 128:(m + 1) * 128].rearrange("(k p) c -> p k c", p=128))])
    NP = len(pieces)
    plen = {pi: PIECE for pi in range(NP)}
    for l in range(L):
        for m in range(8):
            plen[pidx[("w_down", l, m)]] = 22 * 128
    wscr = nc.dram_tensor("wscr", [NP, 128, PIECE], BF16, kind="Internal").ap()

    PE, ACT, DVE, POOL, SP = Eng("pe"), Eng("act"), Eng("dve"), Eng("pool"), Eng("sp")
    engs = [PE, ACT, DVE, POOL, SP]

    with contextlib.ExitStack() as es:
        def sb(name, shape, dt):
            return es.enter_context(nc.sbuf_tensor(name, list(shape), dt))

        for e in engs:
            e.sem = es.enter_context(nc.semaphore("s_" + e.name))

        def dsem(name):
            return DmaSem(es.enter_context(nc.semaphore(name)))

        h4 = sb("h4", (128, NS, KC, SUB), F32)
        xn = sb("xn", (128, NS, KC, SUB), BF16)
        arena = [sb("arena%d" % s, (128, 16896), BF16) for s in range(NS)]
        shr = sb("shr", (128, 7296), BF16)
        sqr = sb("sqr", (128, 4, SUB), BF16)
        rstd = sb("rstd", (128, NS, SUB), F32)
        stg = sb("stg", (128, 2, D), F32)
        KT = sb("KT", (128, L, KC, NMEM), BF16)
        Vt = sb("Vt", (128, L, 2, D), BF16)
        cst = sb("cst", (128, NC_), F32)
        poolw = sb("poolw", (128, L * 4, 128), BF16)
        wmT = sb("wmT", (128, L * 4, 128), BF16)
        ident = sb("ident_sb", (128, 128), F32)
        mask = sb("mask_sb", (128, 128), F32)
        onesD = sb("onesD", (128, 128), BF16)
        ones1 = sb("ones1", (128, 128), BF16)
        hs = sb("hs", (128, 2, L, 22, 2, 2), F32)
        phalo = sb("phalo", (128, L, 4, 16), F32)
        small = sb("small", (128, 256), F32)
        ring = sb("ring", (128, NSLOT, PIECE), BF16)
        banks = [es.enter_context(nc.psum_tensor("bank%d" % i, [128, SUB], F32)) for i in range(8)]
        bankb = [Buf("bank%d" % i) for i in range(8)]
        bank_rr = [0]

        def next_bank():
            i = bank_rr[0]
            bank_rr[0] = (i + 1) % 8
            return banks[i], bankb[i]

        hb = [[Buf("h%d_%d" % (s, k)) for k in range(KC)] for s in range(NS)]
        xnb = [[Buf("xn%d_%d" % (s, k)) for k in range(KC)] for s in range(NS)]
        sqb = [Buf("sq%d" % i) for i in range(4)]
        rstdb = [Buf("rstd%d" % s) for s in range(NS)]
        stgb = [Buf("stg%d" % i) for i in range(2)]
        stghb = [[Buf("stg%d_h%d" % (i, hf)) for hf in range(2)] for i in range(2)]
        stg_st = [dsem("stg_st%d" % i) for i in range(4)]
        KTb = [Buf("KT%d" % l) for l in range(L)]
        Vb = [Buf("V%d" % l) for l in range(L)]
        cstb, poolwb, wmTb, identb, maskb, onesb = Buf("cst"), Buf("poolw"), Buf("wmT"), Buf("ident"), Buf("mask"), Buf("ones")
        hsb = [[[Buf("hs%d_%d_%d" % (q, l, c)) for c in range(NFC)] for l in range(L)] for q in range(2)]
        phb = [Buf("phalo%d" % l) for l in range(L)]
        ringb = [Buf("ring%d" % i) for i in range(NSLOT)]
        ring_sem = [dsem("ring%d" % i) for i in range(NSLOT)]
        csem = dsem("csem")
        sgwsem = dsem("sgwsem")
        memsem = dsem("memsem")
        pwsem = dsem("pwsem")
        conv_sem = [dsem("conv%d" % i) for i in range(NCS)]
        convb = [Buf("convsem%d" % i) for i in range(NCS)]
        scrb = [Buf("scr%d" % i) for i in range(NP)]

        def aview(s, lo, hi, dt=BF16):
            v = arena[s][:, lo // 2:hi // 2]
            return v.bitcast(F32) if dt == F32 else v

        def abuf(name, s, lo, hi):
            return Buf("%s_%d" % (name, s), arena=("arena", s), lo=lo, hi=hi)

        actv = [aview(s, 0, 22528).rearrange("p (k c) -> p k c", k=22) for s in range(NS)]
        actb = [[abuf("act%d" % k, s, k * 1024, (k + 1) * 1024) for k in range(22)] for s in range(NS)]
        cav = [[aview(s, 22528 + i * 2048, 22528 + (i + 1) * 2048, F32) for i in range(4)] for s in range(NS)]
        cab = [[abuf("ca%d" % i, s, 22528 + i * 2048, 22528 + (i + 1) * 2048) for i in range(4)] for s in range(NS)]
        capv = [[aview(s, 22528 + r * 4096, 22528 + (r + 1) * 4096, F32).rearrange("p (g c) -> p g c", g=2) for r in range(2)] for s in range(NS)]
        sgv = [aview(s, 30720, 32768, F32) for s in range(NS)]
        sgb = [abuf("sg", s, 30720, 32768) for s in range(NS)]
        a8v = [aview(s, 0, 8192).rearrange("p (k c) -> p k c", k=8) for s in range(NS)]
        a8b = [[abuf("a8_%d" % k, s, k * 1024, (k + 1) * 1024) for k in range(8)] for s in range(NS)]
        pv = [aview(s, 8192, 8192 + 8448, F32).rearrange("p (g c) -> p g c", g=4) for s in range(NS)]
        pb = [[abuf("p%d" % g, s, 8192 + g * 2112, 8192 + (g + 1) * 2112) for g in range(4)] for s in range(NS)]
        uv = [aview(s, 16640, 20736).rearrange("p (k c) -> p k c", k=4) for s in range(NS)]
        ub = [[abuf("u%d" % k, s, 16640 + k * 1024, 16640 + (k + 1) * 1024) for k in range(4)] for s in range(NS)]
        ddv = [aview(s, 20736, 24832).rearrange("p (k c) -> p k c", k=4) for s in range(NS)]
        ddb = [[abuf("d%d" % k, s, 20736 + k * 1024, 20736 + (k + 1) * 1024) for k in range(4)] for s in range(NS)]
        vnv = [aview(s, 24832, 28928).rearrange("p (k c) -> p k c", k=4) for s in range(NS)]
        vnb = [[abuf("vn%d" % k, s, 24832 + k * 1024, 24832 + (k + 1) * 1024) for k in range(4)] for s in range(NS)]
        xbv = [aview(s, 28928, 33024).rearrange("p (k c) -> p k c", k=4) for s in range(NS)]
        xbb = [[abuf("xb%d" % k, s, 28928 + k * 1024, 28928 + (k + 1) * 1024) for k in range(4)] for s in range(NS)]

        def sview(lo, hi, dt=BF16):
            v = shr[:, lo // 2:hi // 2]
            return v.bitcast(F32) if dt == F32 else v

        def sbuf_(name, lo, hi):
            return Buf(name, arena=("shr", 0), lo=lo, hi=hi)

        ptv = [sview(i * 2112, (i + 1) * 2112, F32) for i in range(2)]
        ptb = [sbuf_("pt%d" % i, i * 2112, (i + 1) * 2112) for i in range(2)]
        vfv = [sview(4224 + i * 2048, 4224 + (i + 1) * 2048, F32) for i in range(3)]
        vfb = [sbuf_("vf%d" % i, 4224 + i * 2048, 4224 + (i + 1) * 2048) for i in range(3)]
        ptv2 = [sview(10368 + i * 2112, 10368 + (i + 1) * 2112, F32) for i in range(2)]
        ptb2 = [sbuf_("pt2_%d" % i, 10368 + i * 2112, 10368 + (i + 1) * 2112) for i in range(2)]
        sgtv = [sview(10368 + i * 2048, 12416 + i * 2048, F32) for i in range(2)]
        sgtb = [sbuf_("sgt%d" % i, 10368 + i * 2048, 12416 + i * 2048) for i in range(2)]
        PTv = [sview(i * 1024, (i + 1) * 1024) for i in range(6)]
        PTb = [sbuf_("PT%d" % i, i * 1024, (i + 1) * 1024) for i in range(6)]
        rdv = [sview(6144 + i * 2048, 6144 + (i + 1) * 2048, F32) for i in range(2)]
        rdb = [sbuf_("rd%d" % i, 6144 + i * 2048, 6144 + (i + 1) * 2048) for i in range(2)]
        NXS = 3
        xsv = sview(0, NXS * 4096, F32).rearrange("p (i f) -> p i f", i=NXS)
        xsb = [sbuf_("xs%d" % i, i * 4096, (i + 1) * 4096) for i in range(NXS)]
        xs_ld = [dsem("xs_ld%d" % i) for i in range(NXS)]
        smallb = [Buf("small%d" % i) for i in range(16)]

        C = lambda off, n=1: cst[:, off:off + n]

        seq = []
        for ti in range(NT):
            for l in range(L):
                seq += [pidx[("w_in", l, 0)], pidx[("w_in", l, 2)], pidx[("w_in", l, 1)]]
                for nm in ("w_out", "wq") + (("wk", "wv") if ti == 0 else ()) + ("wo",):
                    seq += [pidx[(nm, l, 0)], pidx[(nm, l, 1)]]
                seq += [pidx[("w_up", l, i)] for i in range(11)]
                seq += [pidx[("w_down", l, m)] for m in range(8)]
        ws = {"load": 0, "use": 0}

        def ws_load_next():
            n = ws["load"]
            if n >= len(seq):
                return
            ws["load"] = n + 1
            s = n % NSLOT
            pi = seq[n]
            conv_issue_upto(n + 1 + CONV_AHEAD)
            n_el = plen[pi]
            emit_dma(SP, [lambda e, s=s, pi=pi, n_el=n_el: e.dma_start(out=ring[:, s, 0:n_el], in_=wscr[pi][:, 0:n_el])],
                     ring_sem[s], reads=[scrb[pi]], writes=[ringb[s]])

        def ws_acquire(expect, ahead=0):
            n = ws["use"] + ahead
            assert seq[n] == expect, (n, seq[n], expect)
            while ws["load"] <= n:
                ws_load_next()
            s = n % NSLOT
            return ring[:, s, :], ringb[s]

        def ws_release():
            ws["use"] += 1
            while ws["load"] < min(len(seq), ws["use"] + NSLOT):
                ws_load_next()

        def mm_group(pairs, reads, bank=None):
            if bank is None:
                bank = next_bank()
            bt, bb = bank
            n = len(pairs)

            def fn(e):
                for i, (l_, r_) in enumerate(pairs):
                    ins = e.matmul(bt[:, 0:r_.shape[-1]] if r_.shape[-1] != SUB else bt[:], l_, r_, start=(i == 0), stop=(i == n - 1))
                return ins
            emit(PE, fn, reads=reads, writes=[bb], nmm=n)
            return bt, bb

        small_rr = {"ln": [0, 0, 4], "fix": [0, 4, 8], "misc": [0, 12, 4]}

        def next_small(kind="misc"):
            st = small_rr[kind]
            i = st[1] + st[0]
            st[0] = (st[0] + 1) % st[2]
            return small[:, i * 16:(i + 1) * 16], smallb[i]

        emit_dma(SP, [lambda e: e.dma_start(out=cst[:], in_=consts_d),
                      lambda e: e.dma_start(out=ident[:], in_=ident_d),
                      lambda e: e.dma_start(out=mask[:], in_=mask_d)], csem, writes=[cstb, identb, maskb])
        emit(POOL, lambda e: e.memset(onesD[:], 1.0 / D), writes=[onesb])
        emit(POOL, lambda e: e.memset(ones1[:], 1.0), writes=[onesb])
        emit(POOL, lambda e: e.memset(hs[:].rearrange("p a l j g t -> p (a l j g t)"), 0.0),
             writes=[b for q in range(2) for l in range(L) for b in hsb[q][l]])
        emit(POOL, lambda e: e.memset(phalo[:].rearrange("p l g c -> p (l g c)"), 0.0), writes=phb)
        emit_dma(POOL, [lambda e: e.dma_start(out=poolw[:], in_=pool_w_d.rearrange("l g c d -> c (l g) d"))],
                 pwsem, writes=[poolwb])
        order = []
        for pi in seq:
            if pi not in order:
                order.append(pi)
        conv_next = [0]

        def conv_issue_upto(k):
            while conv_next[0] < min(k, len(order)):
                n = conv_next[0]
                pi = order[n]
                fns = [(lambda e, dfn=dfn, src=src, pi=pi: e.dma_start(out=dfn(wscr[pi]), in_=src)) for dfn, src in pieces[pi]]
                emit_dma(POOL, fns, conv_sem[n % NCS], writes=[scrb[pi], convb[n % NCS]])
                conv_next[0] = n + 1

        sgw_v = aview(0, 8192, 8192 + L * 2048, F32).rearrange("p (a s) -> p a s", a=L * 4)
        sgw_b = abuf("sgw", 0, 8192, 8192 + L * 2048)
        emit_dma(SP, [lambda e: e.dma_start(out=sgw_v, in_=sgu_w_d.rearrange("l h t s -> t (l h) s"))], sgwsem, writes=[sgw_b])
        for a0 in range(0, L * 4, 4):
            bt, bb = next_bank()

            def fn(e, a0=a0, bt=bt):
                for j in range(4):
                    ins = e.transpose(bt[:, j * 128:(j + 1) * 128], sgw_v[:, a0 + j, :], ident[:])
                return ins
            emit(PE, fn, reads=[sgw_b, identb], writes=[bb], nmm=4)
            emit(DVE, lambda e, a0=a0, bt=bt: e.tensor_tensor(
                out=wmT[:, a0:a0 + 4, :], in0=bt[:].rearrange("p (a t) -> p a t", a=4),
                in1=mask[:, None, :].broadcast_to([128, 4, 128]), op=ALU.mult), reads=[bb, maskb], writes=[wmTb])

        memv = stg
        emit_dma(SP, [lambda e: e.dma_start(out=stg[:], in_=mem_d.rearrange("(c p) f -> p c f", p=128))], memsem, writes=[stgb[0], stgb[1]])
        for c in range(2):
            sv, sbf = next_small()
            emit(DVE, lambda e, sv=sv, c=c: e.bn_stats(out=sv[:, 0:6], in_=memv[:, c, 0:512]), reads=[stgb[c]], writes=[sbf])
            emit(DVE, lambda e, sv=sv, c=c: e.bn_stats(out=sv[:, 6:12], in_=memv[:, c, 512:1024]), reads=[stgb[c]], writes=[sbf])
            emit(DVE, lambda e, sv=sv: e.bn_aggr(out=sv[:, 12:14], in_=sv[:, 0:12]), reads=[sbf], writes=[sbf])
            emit(DVE, lambda e, sv=sv: e.scalar_tensor_tensor(out=sv[:, 14:15], in0=sv[:, 12:13], scalar=sv[:, 12:13], in1=sv[:, 13:14],
                                                             op0=ALU.mult, op1=ALU.add), reads=[sbf], writes=[sbf])
            emit(ACT, lambda e, sv=sv: e.activation(out=sv[:, 15:16], in_=sv[:, 14:15], func=AF.Sqrt, bias=EPS), reads=[sbf], writes=[sbf])
            emit(DVE, lambda e, sv=sv: e.reciprocal(out=sv[:, 15:16], in_=sv[:, 15:16]), reads=[sbf], writes=[sbf])
            emit(DVE, lambda e, sv=sv, c=c: e.tensor_scalar(out=memv[:, c, :], in0=memv[:, c, :], scalar1=sv[:, 15:16], scalar2=None, op0=ALU.mult),
                 reads=[sbf, stgb[c]], writes=[stgb[c]])
        def kv_compute(l):
            mT_v = aview(0, 8192, 12288).rearrange("p (k m) -> p k m", k=8)
            mT_b = abuf("memnT%d" % l, 0, 8192, 12288)
            for c in range(2):
                for k0 in (0, 4):
                    bt, bb = next_bank()

                    def fn(e, c=c, k0=k0, bt=bt):
                        for j in range(4):
                            ins = e.transpose(bt[:, j * 128:(j + 1) * 128], memv[:, c, (k0 + j) * 128:(k0 + j + 1) * 128], ident[:])
                        return ins
                    emit(PE, fn, reads=[stgb[c], identb], writes=[bb], nmm=4)
                    emit(DVE, lambda e, c=c, k0=k0, bt=bt, l=l, mT_v=mT_v: e.tensor_tensor(
                        out=mT_v[:, k0:k0 + 4, c * 128:(c + 1) * 128], in0=bt[:].rearrange("p (k m) -> p k m", k=4),
                        in1=cst[:, l * LC + O_GMEM + k0:l * LC + O_GMEM + k0 + 4].unsqueeze(2).broadcast_to([128, 4, 128]), op=ALU.mult),
                        reads=[bb, cstb], writes=[mT_b])
            for pc in range(2):
                slot, slb = ws_acquire(pidx[("wk", l, pc)])
                s3 = slot.rearrange("p (k c) -> p k c", k=8)
                for mm in range(4):
                    dch = pc * 4 + mm
                    bt, bb = mm_group([(s3[:, k, mm * 128:(mm + 1) * 128], mT_v[:, k, :]) for k in range(8)], reads=[slb, mT_b])
                    emit(ACT, lambda e, bt=bt, l=l, dch=dch: e.activation(out=KT[:, l, dch, :], in_=bt[:, 0:NMEM], func=AF.Copy),
                         reads=[bb], writes=[KTb[l]])
                ws_release()
            for pc in range(2):
                slot, slb = ws_acquire(pidx[("wv", l, pc)])
                s3 = slot.rearrange("p (k c) -> p k c", k=8)
                for mc in range(2):
                    bt, bb = mm_group([(mT_v[:, k, mc * 128:(mc + 1) * 128], s3[:, k, :]) for k in range(8)], reads=[slb, mT_b])
                    emit(ACT, lambda e, bt=bt, l=l, mc=mc, pc=pc: e.activation(out=Vt[:, l, mc, pc * 512:(pc + 1) * 512], in_=bt[:], func=AF.Copy),
                         reads=[bb], writes=[Vb[l]])
                ws_release()

        sq_rr = [0]

        def norm_stats(s):
            bank = next_bank()
            bt, bb = bank
            for k in range(KC):
                i = sq_rr[0]
                sq_rr[0] = (i + 1) % 4
                emit(ACT, lambda e, i=i, k=k: e.activation(out=sqr[:, i, :], in_=h4[:, s, k, :], func=AF.Square),
                     reads=[hb[s][k]], writes=[sqb[i]])
                emit(PE, lambda e, i=i, k=k: e.matmul(bt[:], onesD[:], sqr[:, i, :], start=(k == 0), stop=(k == KC - 1)),
                     reads=[sqb[i], onesb], writes=[bb], nmm=1)
            emit(ACT, lambda e: e.activation(out=rstd[:, s, :], in_=bt[:], func=AF.Ln, bias=C(O_EPS)), reads=[bb, cstb], writes=[rstdb[s]])
            emit(ACT, lambda e: e.activation(out=rstd[:, s, :], in_=rstd[:, s, :], func=AF.Exp, scale=-0.5), reads=[rstdb[s]], writes=[rstdb[s]])

        xn_pend = {}

        def xn_emit(s, n):
            st = xn_pend.get(s)
            while st is not None and n > 0 and st[1] < KC:
                goff, k = st
                emit(DVE, lambda e, k=k, goff=goff: e.scalar_tensor_tensor(out=xn[:, s, k, :], in0=h4[:, s, k, :], scalar=C(goff + k),
                                                                           in1=rstd[:, s, :], op0=ALU.mult, op1=ALU.mult),
                     reads=[hb[s][k], rstdb[s], cstb], writes=[xnb[s][k]])
                st[1] += 1
                n -= 1
            if st is not None and st[1] >= KC:
                del xn_pend[s]

        def xn_flush(s):
            xn_emit(s, KC)

        def norm_begin(s, goff):
            norm_stats(s)
            xn_pend[s] = [goff, 0]

        def load_x_dma(ti, tcg):
            i = tcg % NXS
            r0 = ti * T + tcg * 128
            emit_dma(SP, [lambda e, i=i, r0=r0: e.dma_start(out=xsv[:, i, :], in_=x_d[r0:r0 + 128, :])], xs_ld[i], writes=[xsb[i]])

        def load_x_chunk(ti, tcg, after):
            s, tc = divmod(tcg, SUB // 128)
            i = tcg % NXS
            b0, b1 = next_bank(), next_bank()

            def fn(e, i=i, b0=b0, b1=b1):
                for kc in range(KC):
                    bt = (b0 if kc < 4 else b1)[0]
                    ins = e.transpose(bt[:, (kc % 4) * 128:(kc % 4 + 1) * 128], xsv[:, i, kc * 128:(kc + 1) * 128], ident[:])
                return ins
            emit(PE, fn, reads=[xsb[i], identb], writes=[b0[1], b1[1]], nmm=8)
            if tcg + NXS < T // 128:
                load_x_dma(ti, tcg + NXS)
            emit(ACT, lambda e: e.activation(out=h4[:, s, 0:4, tc * 128:(tc + 1) * 128],
                                             in_=b0[0][:].rearrange("p (k t) -> p k t", k=4), func=AF.Copy),
                 reads=[b0[1]], writes=hb[s][0:4])
            emit(DVE, lambda e: e.tensor_copy(out=h4[:, s, 4:8, tc * 128:(tc + 1) * 128],
                                              in_=b1[0][:].rearrange("p (k t) -> p k t", k=4)),
                 reads=[b1[1]], writes=hb[s][4:8])
            if s == 1:
                xn_emit(0, 2)
            if tc == SUB // 128 - 1:
                after(s)

        vf_rr = [0]

        def mixer_p(s, l, slot, slb, first, between=None):
            s3 = slot.rearrange("p (k c) -> p k c", k=8)
            emit(POOL, lambda e: e.tensor_copy(out=pv[s][:, :, 0:16], in_=phalo[:, l, :, :]), reads=[phb[l]], writes=pb[s])
            for m in range(4):
                bt, bb = mm_group([(s3[:, k, m * 128:(m + 1) * 128], xn[:, s, k, :]) for k in range(KC)], reads=[slb] + xnb[s])
                emit(ACT, lambda e, bt=bt, m=m: e.activation(out=pv[s][:, m, 16:528], in_=bt[:], func=AF.Identity, scale=1.0 / WINS[m], bias=0.0),
                     reads=[bb], writes=[pb[s][m]])
                emit(ACT, lambda e, bt=bt, m=m: e.activation(out=xbv[s][:, m, :], in_=bt[:], func=AF.Copy), reads=[bb], writes=[xbb[s][m]])
                if between is not None:
                    between()
            emit(POOL, lambda e: e.tensor_copy(out=phalo[:, l, :, :], in_=pv[s][:, :, 512:528]), reads=pb[s], writes=[phb[l]])

        pool_q = []

        def pool_drain(n=10 ** 6):
            while pool_q and n > 0:
                pool_q.pop(0)()
                n -= 1

        def pooling(s, l, first, POOL=POOL, tv=None, tb=None, queue=False):
            if queue:
                real_emit = emit

                def q_emit(E, fn, reads=(), writes=()):
                    pool_q.append(lambda: real_emit(E, fn, reads=reads, writes=writes))
                return _pooling(s, l, first, POOL, tv, tb, q_emit)
            return _pooling(s, l, first, POOL, tv, tb, emit)

        def _pooling(s, l, first, POOL, tv, tb, emit):
            tv = ptv if tv is None else tv
            tb = ptb if tb is None else tb
            for g, win in enumerate(WINS):
                X = pv[s][:, g, :]
                t1, t2 = tv[0], tv[1]
                steps = [(1, t1, tb[0]), (2, t2, tb[1]), (4, t1, tb[0]), (8, t2, tb[1])][:g + 1]
                src, srcb, lo = X, pb[s][g], 0
                for (sh, dst, dstb) in steps:
                    nlo = lo + sh
                    emit(POOL, lambda e, src=src, dst=dst, nlo=nlo, sh=sh: e.tensor_tensor(
                        out=dst[:, nlo:528], in0=src[:, nlo:528], in1=src[:, nlo - sh:528 - sh], op=ALU.add),
                        reads=[srcb], writes=[dstb])
                    src, srcb, lo = dst, dstb, nlo
                emit(POOL, lambda e, src=src, g=g: e.tensor_tensor(out=ddv[s][:, g, :], in0=src[:, 16:528], in1=xbv[s][:, g, :], op=ALU.subtract),
                     reads=[srcb, xbb[s][g]], writes=[ddb[s][g]])
                if first:
                    sv, sbf = next_small()
                    emit(POOL, lambda e, src=src, sv=sv, g=g: e.tensor_tensor(out=sv[:, 0:16], in0=src[:, 16:32], in1=C(O_INV + g * 16, 16), op=ALU.mult),
                         reads=[srcb, cstb], writes=[sbf])
                    emit(POOL, lambda e, sv=sv, g=g: e.tensor_tensor(out=ddv[s][:, g, 0:16], in0=sv[:, 0:16], in1=xbv[s][:, g, 0:16], op=ALU.subtract),
                         reads=[sbf, xbb[s][g]], writes=[ddb[s][g]])

        def mixer_v(s, l, slot, slb):
            s3 = slot.rearrange("p (k c) -> p k c", k=8)
            pend = []

            def ln_rstd(p_):
                sv, sbf, i, tc = p_
                emit(ACT, lambda e: e.activation(out=sv[:, 8:9], in_=sv[:, 7:8], func=AF.Sqrt, bias=C(O_EPS)), reads=[sbf, cstb], writes=[sbf])

            def normalize(p_):
                sv, sbf, i, tc = p_
                emit(DVE, lambda e: e.reciprocal(out=sv[:, 9:10], in_=sv[:, 8:9]), reads=[sbf], writes=[sbf])
                emit(DVE, lambda e: e.tensor_scalar(out=vnv[s][:, tc, :], in0=vfv[i], scalar1=sv[:, 6:7], scalar2=sv[:, 9:10],
                                                    op0=ALU.subtract, op1=ALU.mult), reads=[sbf, vfb[i]], writes=[vnb[s][tc]])
            for tc in range(4):
                bt, bb = mm_group([(xn[:, s, k, tc * 128:(tc + 1) * 128], s3[:, k, :]) for k in range(KC)], reads=[slb] + xnb[s])
                i = vf_rr[0]
                vf_rr[0] = (i + 1) % 3
                emit(ACT, lambda e, bt=bt, i=i: e.activation(out=vfv[i], in_=bt[:], func=AF.Gelu), reads=[bb], writes=[vfb[i]])
                if pend:
                    ln_rstd(pend[-1])
                sv, sbf = next_small("ln")
                emit(DVE, lambda e, sv=sv, i=i: e.bn_stats(out=sv[:, 0:6], in_=vfv[i]), reads=[vfb[i]], writes=[sbf])
                emit(DVE, lambda e, sv=sv: e.bn_aggr(out=sv[:, 6:8], in_=sv[:, 0:6]), reads=[sbf], writes=[sbf])
                if pend:
                    normalize(pend.pop(0))
                pend.append((sv, sbf, i, tc))
                pool_drain(2)
            ln_rstd(pend[-1])
            normalize(pend.pop(0))

        def mixer_u(s, l, slot, slb):
            s3 = slot.rearrange("p (k c) -> p k c", k=8)
            for m in range(4):
                bt, bb = mm_group([(s3[:, k, m * 128:(m + 1) * 128], xn[:, s, k, :]) for k in range(KC)], reads=[slb] + xnb[s])
                emit(ACT, lambda e, bt=bt, m=m: e.activation(out=uv[s][:, m, :], in_=bt[:], func=AF.Gelu), reads=[bb], writes=[ub[s][m]])

        def mixer_mix(s, l):
            base = l * LC
            for g in range(4):
                bt, bb = mm_group([(poolw[:, l * 4 + g, :], ddv[s][:, g, :])], reads=[poolwb, ddb[s][g]])
                emit(ACT, lambda e, bt=bt, g=g: e.activation(out=a8v[s][:, g, :], in_=bt[:], func=AF.Identity, scale=C(base + O_PSC + g), bias=0.0),
                     reads=[bb, cstb], writes=[a8b[s][g]])
            for hh in range(4):
                bt, bb = next_bank()

                def fn(e, bt=bt, hh=hh):
                    for tc in range(4):
                        ins = e.matmul(bt[:, tc * 128:(tc + 1) * 128], vnv[s][:, tc, hh * 128:(hh + 1) * 128], wmT[:, l * 4 + hh, :], start=True, stop=True)
                    return ins
                emit(PE, fn, reads=vnb[s] + [wmTb], writes=[bb], nmm=4)
                gi = hh % 2
                emit(DVE, lambda e, bt=bt, hh=hh, gi=gi: e.scalar_tensor_tensor(
                    out=sgtv[gi].rearrange("p (r t) -> p r t", r=4), in0=bt[:].rearrange("p (r t) -> p r t", r=4), scalar=C(base + O_SGG + hh),
                    in1=cst[:, base + O_BBC + hh * 128:base + O_BBC + (hh + 1) * 128][:, None, :].broadcast_to([128, 4, 128]),
                    op0=ALU.mult, op1=ALU.add), reads=[bb, cstb], writes=[sgtb[gi]])
                emit(POOL, lambda e, hh=hh, gi=gi: e.tensor_tensor(out=a8v[s][:, 4 + hh, :], in0=sgtv[gi], in1=uv[s][:, hh, :], op=ALU.mult),
                     reads=[sgtb[gi], ub[s][hh]], writes=[a8b[s][4 + hh]])

        def proj_resid(s, pc, slot, slb, between=None):
            s3 = slot.rearrange("p (k c) -> p k c", k=8)
            for mm in range(4):
                m = pc * 4 + mm
                bt, bb = mm_group([(s3[:, k, mm * 128:(mm + 1) * 128], a8v[s][:, k, :]) for k in range(KC)], reads=[slb] + a8b[s])
                emit(DVE, lambda e, bt=bt, m=m: e.tensor_tensor(out=h4[:, s, m, :], in0=bt[:], in1=h4[:, s, m, :], op=ALU.add),
                     reads=[bb, hb[s][m]], writes=[hb[s][m]])
                if between is not None:
                    between()

        def proj_q(s, pc, slot, slb, between=None):
            s3 = slot.rearrange("p (k c) -> p k c", k=8)
            for mm in range(4):
                m = pc * 4 + mm
                bt, bb = mm_group([(s3[:, k, mm * 128:(mm + 1) * 128], xn[:, s, k, :]) for k in range(KC)], reads=[slb] + xnb[s])
                emit(ACT, lambda e, bt=bt, m=m: e.activation(out=a8v[s][:, m, :], in_=bt[:], func=AF.Copy), reads=[bb], writes=[a8b[s][m]])
                if between is not None:
                    between()

        def attention_all(l):
            units = [(s, hh) for hh in range(4) for s in range(NS)]

            def scores(u):
                s, hh = units[u]
                r = u % 3
                for mc in range(2):
                    bt, bb = mm_group([(KT[:, l, 2 * hh + dc, mc * 128:(mc + 1) * 128], a8v[s][:, 2 * hh + dc, :]) for dc in range(2)],
                                      reads=[KTb[l], a8b[s][2 * hh], a8b[s][2 * hh + 1]])
                    emit(ACT, lambda e, bt=bt, r=r, mc=mc: e.activation(out=PTv[2 * r + mc], in_=bt[:], func=AF.Exp, scale=1.0 / 16.0),
                         reads=[bb], writes=[PTb[2 * r + mc]])

            def rest(u):
                s, hh = units[u]
                r = u % 3
                rr = u % 2
                bt, bb = mm_group([(ones1[:], PTv[2 * r + mc]) for mc in range(2)], reads=[onesb, PTb[2 * r], PTb[2 * r + 1]])
                emit(ACT, lambda e, bt=bt, rr=rr: e.activation(out=rdv[rr], in_=bt[:], func=AF.Ln), reads=[bb], writes=[rdb[rr]])
                emit(ACT, lambda e, rr=rr: e.activation(out=rdv[rr], in_=rdv[rr], func=AF.Exp, scale=-1.0), reads=[rdb[rr]], writes=[rdb[rr]])
                for dc in range(2):
                    dch = 2 * hh + dc
                    bt, bb = mm_group([(Vt[:, l, mc, dch * 128:(dch + 1) * 128], PTv[2 * r + mc]) for mc in range(2)],
                                      reads=[Vb[l], PTb[2 * r], PTb[2 * r + 1]])
                    emit(DVE, lambda e, bt=bt, rr=rr, dch=dch, s=s: e.tensor_tensor(out=a8v[s][:, dch, :], in0=bt[:], in1=rdv[rr], op=ALU.mult),
                         reads=[bb, rdb[rr]], writes=[a8b[s][dch]])

            n = len(units)
            scores(0)
            scores(1)
            for u in range(n):
                if u + 2 < n:
                    scores(u + 2)
                rest(u)

        ca_rr = [0, 0]
        hs_par = {}

        ffn_pend = []

        def ffn_flush():
            while ffn_pend:
                ffn_pend.pop(0)()

        def ffn_up(s, l, i, slot, slb, between=None):
            base = l * LC
            s3 = slot.rearrange("p (k c) -> p k c", k=8)
            for j in range(2):
                jj = 2 * i + j
                r = ca_rr[s]
                ca_rr[s] ^= 1
                q = hs_par.get((l, jj), 0)
                hs_par[(l, jj)] = q ^ 1
                res = []
                for which, coff in ((0, j * 128), (1, 256 + j * 128)):
                    cf = jj + 22 * which
                    cc = 2 * jj + which
                    bt, bb = mm_group([(s3[:, k, coff:coff + 128], xn[:, s, k, :]) for k in range(KC)], reads=[slb] + xnb[s])
                    av, ab = cav[s][2 * r + which], cab[s][2 * r + which]
                    emit(ACT, lambda e, bt=bt, av=av, cc=cc: e.activation(out=av, in_=bt[:], func=AF.Identity, scale=C(base + O_CW2 + cc), bias=C(base + O_CB + cc)),
                         reads=[bb, cstb], writes=[ab])
                    emit(ACT, lambda e, bt=bt, jj=jj, which=which, q=q: e.activation(out=hs[:, q ^ 1, l, jj, which, :], in_=bt[:, 510:512], func=AF.Copy),
                         reads=[bb], writes=[hsb[q ^ 1][l][cf]])
                    emit(DVE, lambda e, bt=bt, av=av, cc=cc: e.scalar_tensor_tensor(out=av[:, 1:512], in0=bt[:, 0:511], scalar=C(base + O_CW1 + cc),
                                                                                 in1=av[:, 1:512], op0=ALU.mult, op1=ALU.add),
                         reads=[bb, ab, cstb], writes=[ab])
                    emit(DVE, lambda e, bt=bt, av=av, cc=cc: e.scalar_tensor_tensor(out=av[:, 2:512], in0=bt[:, 0:510], scalar=C(base + O_CW0 + cc),
                                                                                 in1=av[:, 2:512], op0=ALU.mult, op1=ALU.add),
                         reads=[bb, ab, cstb], writes=[ab])
                    res.append((av, ab))
                (ag, agb), (avv, avb) = res
                sv, sbf = next_small("fix")
                hsp = hs[:, q, l, jj, :, :]
                hsrd = [hsb[q][l][jj], hsb[q][l][22 + jj]]
                cw0p = cst[:, base + O_CW0 + 2 * jj:base + O_CW0 + 2 * jj + 2].unsqueeze(2)
                cw1p = cst[:, base + O_CW1 + 2 * jj:base + O_CW1 + 2 * jj + 2].unsqueeze(2)
                ap_ = capv[s][r]
                emit(POOL, lambda e, sv=sv, hsp=hsp, cw0p=cw0p: e.tensor_tensor(out=sv[:, 0:4].rearrange("p (g t) -> p g t", g=2), in0=hsp,
                                                                               in1=cw0p.broadcast_to([128, 2, 2]), op=ALU.mult),
                     reads=hsrd + [cstb], writes=[sbf])
                emit(POOL, lambda e, sv=sv, hsp=hsp, cw1p=cw1p: e.tensor_tensor(out=sv[:, 4:6].rearrange("p (g t) -> p g t", g=2), in0=hsp[:, :, 1:2],
                                                                               in1=cw1p, op=ALU.mult),
                     reads=hsrd + [cstb], writes=[sbf])
                emit(POOL, lambda e, sv=sv, ap_=ap_: e.tensor_tensor(out=ap_[:, :, 0:2], in0=ap_[:, :, 0:2], in1=sv[:, 0:4].rearrange("p (g t) -> p g t", g=2), op=ALU.add),
                     reads=[sbf, agb, avb], writes=[agb, avb])
                emit(POOL, lambda e, sv=sv, ap_=ap_: e.tensor_tensor(out=ap_[:, :, 0:1], in0=ap_[:, :, 0:1], in1=sv[:, 4:6].rearrange("p (g t) -> p g t", g=2), op=ALU.add),
                     reads=[sbf, agb, avb], writes=[agb, avb])

                def finish(ag=ag, agb=agb, avv=avv, avb=avb, jj=jj, s=s):
                    emit(ACT, lambda e: e.activation(out=sgv[s], in_=ag, func=AF.Silu), reads=[agb], writes=[sgb[s]])
                    emit(POOL, lambda e: e.tensor_tensor(out=actv[s][:, jj, :], in0=sgv[s], in1=avv, op=ALU.mult),
                         reads=[sgb[s], avb], writes=[actb[s][jj]])
                ffn_flush()
                ffn_pend.append(finish)
                if between is not None:
                    between()
                    between()

        def ffn_down(s, m, slot, slb, pre_evac=None, k0=0, k1=22):
            s3 = slot[:, 0:2816].rearrange("p (k c) -> p k c", k=22)
            bt, bb = mm_group([(s3[:, k, :], actv[s][:, k, :]) for k in range(k0, k1)], reads=[slb] + actb[s][k0:k1])
            if pre_evac is not None:
                pre_evac()
            emit(DVE, lambda e, bt=bt: e.tensor_tensor(out=h4[:, s, m, :], in0=bt[:], in1=h4[:, s, m, :], op=ALU.add),
                 reads=[bb, hb[s][m]], writes=[hb[s][m]])

        out_i = [0]

        def final_scale(s):
            for k in range(KC):
                emit(DVE, lambda e, k=k: e.scalar_tensor_tensor(out=h4[:, s, k, :], in0=h4[:, s, k, :], scalar=C(O_GFIN + k),
                                                              in1=rstd[:, s, :], op0=ALU.mult, op1=ALU.mult),
                     reads=[hb[s][k], rstdb[s], cstb], writes=[hb[s][k]])

        def final_store_chunk(s, ti, tc):
            i = out_i[0]
            out_i[0] ^= 1
            b0, b1 = next_bank(), next_bank()

            def fn(e, b0=b0, b1=b1, tc=tc):
                for kc in range(KC):
                    bt = (b0 if kc < 4 else b1)[0]
                    ins = e.transpose(bt[:, (kc % 4) * 128:(kc % 4 + 1) * 128], h4[:, s, kc, tc * 128:(tc + 1) * 128], ident[:])
                return ins
            emit(PE, fn, reads=hb[s] + [identb], writes=[b0[1], b1[1]], nmm=8)
            r0 = ti * T + s * SUB + tc * 128
            emit(ACT, lambda e: e.activation(out=stg[:, i, 0:512], in_=b0[0][:], func=AF.Copy), reads=[b0[1]], writes=[stgb[i], stghb[i][0]])
            emit_dma(ACT, [lambda e: e.dma_start(out=out_d[r0:r0 + 128, 0:512], in_=stg[:, i, 0:512])], stg_st[2 * i], reads=[stghb[i][0]])
            emit(DVE, lambda e: e.tensor_copy(out=stg[:, i, 512:1024], in_=b1[0][:]), reads=[b1[1]], writes=[stgb[i], stghb[i][1]])
            emit_dma(SP, [lambda e: e.dma_start(out=out_d[r0:r0 + 128, 512:1024], in_=stg[:, i, 512:1024])], stg_st[2 * i + 1], reads=[stghb[i][1]])

        def piece_step(key, fn, after=None):
            STAGE[0] = "%s%s" % (key[0], key[2] if key[0] in ("w_in",) else "")
            slot, slb = ws_acquire(pidx[key])
            for s in range(NS):
                fn(s, slot, slb)
                if after is not None:
                    after(s)
            ws_release()

        def xn_flush_0():
            xn_flush(0)

        def final_norm(s):
            norm_stats(s)
            final_scale(s)

        def norm_piece(key, fn, goff):
            STAGE[0] = key[0]
            slot, slb = ws_acquire(pidx[key])
            fn(0, slot, slb, None)
            norm_begin(0, goff)
            fn(1, slot, slb, lambda: xn_emit(0, 2))
            xn_flush(0)
            norm_begin(1, goff)
            xn_flush(1)
            ws_release()

        def first_piece(key, fn):
            STAGE[0] = "%s%s" % (key[0], key[2] if key[0] in ("w_in",) else "")
            slot, slb = ws_acquire(pidx[key])
            fn(0, slot, slb, lambda: xn_emit(1, 1))
            xn_flush(1)
            fn(1, slot, slb, None)
            ws_release()

        norm1_first = lambda s: norm_begin(s, O_GMIX)
        for ti in range(NT):
            STAGE[0] = "load_x"
            if ti == 0:
                for c_ in range(NXS):
                    load_x_dma(0, c_)
                for tcg in range(4):
                    load_x_chunk(0, tcg, norm1_first)
            for tcg in range(4, 8):
                load_x_chunk(ti, tcg, norm1_first)
            xn_flush(0)
            xn_flush(1)
            for l in range(L):
                base = l * LC
                first_piece(("w_in", l, 0), lambda s, slot, slb, bw: mixer_p(s, l, slot, slb, first=(ti == 0 and s == 0), between=bw))
                pooling(0, l, first=(ti == 0))
                pooling(1, l, first=False, POOL=DVE, tv=ptv2, tb=ptb2, queue=True)
                piece_step(("w_in", l, 2), lambda s, slot, slb: mixer_v(s, l, slot, slb))
                pool_drain()
                piece_step(("w_in", l, 1), lambda s, slot, slb: mixer_u(s, l, slot, slb), after=lambda s: mixer_mix(s, l))
                piece_step(("w_out", l, 0), lambda s, slot, slb: proj_resid(s, 0, slot, slb))
                norm_piece(("w_out", l, 1), lambda s, slot, slb, bw: proj_resid(s, 1, slot, slb, between=bw), base + O_GX)
                first_piece(("wq", l, 0), lambda s, slot, slb, bw: proj_q(s, 0, slot, slb, between=bw))
                piece_step(("wq", l, 1), lambda s, slot, slb: proj_q(s, 1, slot, slb))
                if ti == 0:
                    STAGE[0] = "kv"
                    kv_compute(l)
                STAGE[0] = "attn"
                attention_all(l)
                piece_step(("wo", l, 0), lambda s, slot, slb: proj_resid(s, 0, slot, slb))
                norm_piece(("wo", l, 1), lambda s, slot, slb, bw: proj_resid(s, 1, slot, slb, between=bw), base + O_GFFN)
                first_piece(("w_up", l, 0), lambda s, slot, slb, bw: ffn_up(s, l, 0, slot, slb, between=bw))
                for i in range(1, 11):
                    piece_step(("w_up", l, i), lambda s, slot, slb: ffn_up(s, l, i, slot, slb))
                ffn_flush()
                STAGE[0] = "w_down"
                sl0 = ws_acquire(pidx[("w_down", l, 0)])
                sl1 = ws_acquire(pidx[("w_down", l, 1)], ahead=1)
                for (k0, k1) in ((0, 16), (16, 22)):
                    for m, (slot, slb) in ((0, sl0), (1, sl1)):
                        for s in range(NS):
                            ffn_down(s, m, slot, slb, k0=k0, k1=k1)
                ws_release()
                ws_release()
                for m in range(2, 7):
                    piece_step(("w_down", l, m), lambda s, slot, slb: ffn_down(s, m, slot, slb))
                if l + 1 < L:
                    norm_piece(("w_down", l, 7), lambda s, slot, slb, bw: ffn_down(s, 7, slot, slb, pre_evac=(None if bw is None else xn_flush_0)),
                               (l + 1) * LC + O_GMIX)
                else:
                    if ti + 1 < NT:
                        for c_ in range(NXS):
                            load_x_dma(ti + 1, c_)
                    piece_step(("w_down", l, 7), lambda s, slot, slb: ffn_down(s, 7, slot, slb), after=final_norm)
            STAGE[0] = "final"
            for tc in range(4):
                final_store_chunk(0, ti, tc)
            for tc in range(4):
                final_store_chunk(1, ti, tc)
                if ti + 1 < NT:
                    load_x_chunk(ti + 1, tc, norm1_first)

        for i in range(4):
            if stg_st[i].val:
                ACT.ops.append(lambda e, i=i: e.wait_ge(stg_st[i].sem, stg_st[i].val))

        with nc.Block() as block:
            @block.tensor
            def _(e):
                for op in PE.ops:
                    op(e)

            @block.scalar
            def _(e):
                for op in ACT.ops:
                    op(e)

            @block.vector
            def _(e):
                for op in DVE.ops:
                    op(e)

            @block.gpsimd
            def _(e):
                for op in POOL.ops:
                    op(e)

            @block.sync
            def _(e):
                for op in SP.ops:
                    op(e)
    return nc


def make_consts(L, norm_mix_g, norm_xattn_g, norm_ffn_g, mem_norm_g, pool_scale, sgu_g, conv_w, conv_b, sgu_b, final_norm_g):
    NC_ = L * LC + 8 + 64 + 1
    c = np.zeros((128, NC_), np.float32)
    col = lambda v: np.asarray(v, np.float32).reshape(-1, 128).T
    for l in range(L):
        b = l * LC
        c[:, b + O_GMIX:b + O_GMIX + 8] = col(norm_mix_g[l])
        c[:, b + O_GX:b + O_GX + 8] = col(norm_xattn_g[l])
        c[:, b + O_GFFN:b + O_GFFN + 8] = col(norm_ffn_g[l])
        c[:, b + O_GMEM:b + O_GMEM + 8] = col(mem_norm_g[l])
        c[:, b + O_PSC:b + O_PSC + 4] = col(pool_scale[l])
        c[:, b + O_SGG:b + O_SGG + 4] = col(sgu_g[l])
        pair = lambda v: col(v).reshape(128, 2, 22).transpose(0, 2, 1).reshape(128, 44)
        c[:, b + O_CW0:b + O_CW0 + 44] = pair(conv_w[l, 0])
        c[:, b + O_CW1:b + O_CW1 + 44] = pair(conv_w[l, 1])
        c[:, b + O_CW2:b + O_CW2 + 44] = pair(conv_w[l, 2])
        c[:, b + O_CB:b + O_CB + 44] = pair(conv_b[l])
        c[:, b + O_BBC:b + O_BBC + 512] = np.broadcast_to(np.asarray(sgu_b[l], np.float32).reshape(1, 512), (128, 512))
    c[:, L * LC:L * LC + 8] = col(final_norm_g)
    inv = np.zeros((4, 16), np.float32)
    for g, win in enumerate(WINS):
        inv[g] = float(win) / np.minimum(np.arange(16) + 1, win)
    c[:, L * LC + 8:L * LC + 72] = np.broadcast_to(inv.reshape(1, 64), (128, 64))
    c[:, L * LC + 72] = EPS
    return c


_cache = {}


def run(inputs, S, L, n_cores):
    f = lambda a: np.ascontiguousarray(np.asarray(a, dtype=np.float32))
    key = (S, L)
    if key not in _cache:
        _cache[key] = build(S, L)
    nc = _cache[key]
    consts = make_consts(L, *[np.asarray(inputs[k], np.float32) for k in
                              ("norm_mix_g", "norm_xattn_g", "norm_ffn_g", "mem_norm_g", "pool_scale", "sgu_g", "conv_w", "conv_b", "sgu_b", "final_norm_g")])
    ident = np.eye(128, dtype=np.float32)
    tril = np.triu(np.ones((128, 128), np.float32))
    shared = {k: f(inputs[k]) for k in ("w_in", "pool_w", "sgu_w", "w_out", "wq", "wk", "wv", "wo", "w_up", "w_down")}
    shared.update(consts=consts, ident=ident, trilmask=tril)
    x = f(inputs["x"])
    mem = f(inputs["mem"])
    in_maps = []
    for c in range(n_cores):
        m = dict(shared)
        m["x"] = x[c]
        m["mem"] = mem[c]
        in_maps.append(m)
    res = run_bass_kernel_spmd(nc, in_maps, core_ids=list(range(n_cores)))
    return np.stack([np.asarray(r["out"], dtype=np.float32) for r in res.results], axis=0)


def kernel(**inputs):
    return run(inputs, 8192, 2, 8)
```

```python
import contextlib
import numpy as np
import concourse.bass as bass
import concourse.mybir as mybir
from concourse.bass_utils import run_bass_kernel_spmd

F32 = mybir.dt.float32
BF16 = mybir.dt.bfloat16
AF = mybir.ActivationFunctionType
ALU = mybir.AluOpType

D = 1024
KC = 8
NMEM = 256
DFF = 2816
NFC = 44
SUB = 512
NS = 2
T = SUB * NS
EPS = 1e-6
WINS = (2, 4, 8, 16)
NSLOT = 4
CONV_AHEAD = 6
NCS = 12
PIECE = 4096

LC = 728
O_GMIX, O_GX, O_GFFN, O_GMEM, O_PSC, O_SGG, O_CW0, O_CW1, O_CW2, O_CB, O_BBC = 0, 8, 16, 24, 32, 36, 40, 84, 128, 172, 216


class Eng:
    def __init__(self, name):
        self.name = name
        self.sem = None
        self.cnt = 0
        self.known = {}
        self.ops = []


class Buf:
    registry = {}

    def __init__(self, name, arena=None, lo=0, hi=0):
        self.name = name
        self.w = None
        self.r = {}
        self.arena = arena
        self.lo, self.hi = lo, hi
        self.al = []
        if arena is not None:
            lst = Buf.registry.setdefault(arena, [])
            for o in lst:
                if o.lo < hi and lo < o.hi:
                    o.al.append(self)
                    self.al.append(o)
            lst.append(self)


def _need(E, waits, ev, same_ok):
    if ev is None:
        return
    sem, val, owner = ev
    if same_ok and owner is E and E.name == "pe":
        return
    k = id(sem)
    if E.known.get(k, 0) >= val:
        return
    if k not in waits or waits[k][1] < val:
        waits[k] = (sem, val)


def _collect(E, reads, writes):
    waits = {}
    for b in reads:
        _need(E, waits, b.w, False)
    for b in writes:
        for x in [b] + b.al:
            _need(E, waits, x.w, True)
            for ev in x.r.values():
                _need(E, waits, ev, True)
    for k, (sem, val) in waits.items():
        E.known[k] = val
        E.ops.append(lambda e, s=sem, v=val: e.wait_ge(s, v))


STAGE = ["init"]
PE_LOG = []


def emit(E, fn, reads=(), writes=(), nmm=0):
    if nmm:
        PE_LOG.append((STAGE[0], nmm))
    _collect(E, reads, writes)
    E.cnt += 1
    ev = (E.sem, E.cnt, E)
    sem = E.sem
    E.ops.append(lambda e: fn(e).then_inc(sem, 1))
    for b in reads:
        b.r[id(sem)] = ev
    for b in writes:
        b.w = ev
        b.r = {}
    return ev


class DmaSem:
    def __init__(self, sem):
        self.sem = sem
        self.val = 0


def emit_dma(Q, fns, dsem, reads=(), writes=()):
    _collect(Q, reads, writes)
    dsem.val += 16 * len(fns)
    ev = (dsem.sem, dsem.val, None)
    s = dsem.sem
    for fn in fns:
        Q.ops.append(lambda e, fn=fn: fn(e).then_inc(s, 16))
    for b in reads:
        b.r[id(s)] = ev
    for b in writes:
        b.w = ev
        b.r = {}
    return ev


def build(S, L):
    NT = S // T
    del PE_LOG[:]
    STAGE[0] = "init"
    Buf.registry = {}
    nc = bass.Bass("TRN2", target_bir_lowering=False)
    dt_in = lambda name, shape: nc.dram_tensor(name, list(shape), F32, kind="ExternalInput").ap()
    x_d = dt_in("x", (S, D))
    mem_d = dt_in("mem", (NMEM, D))
    w_in_d = dt_in("w_in", (L, D, 1536))
    pool_w_d = dt_in("pool_w", (L, 4, 128, 128))
    sgu_w_d = dt_in("sgu_w", (L, 4, 128, 128))
    w_out_d = dt_in("w_out", (L, D, D))
    wq_d = dt_in("wq", (L, D, D))
    wk_d = dt_in("wk", (L, D, D))
    wv_d = dt_in("wv", (L, D, D))
    wo_d = dt_in("wo", (L, D, D))
    w_up_d = dt_in("w_up", (L, D, 2 * DFF))
    w_down_d = dt_in("w_down", (L, DFF, D))
    NC_ = L * LC + 8 + 64 + 1
    O_GFIN = L * LC
    O_INV = L * LC + 8
    O_EPS = L * LC + 72
    consts_d = dt_in("consts", (128, NC_))
    ident_d = dt_in("ident", (128, 128))
    mask_d = dt_in("trilmask", (128, 128))
    out_d = nc.dram_tensor("out", [S, D], F32, kind="ExternalOutput").ap()

    pieces = []

    def colpiece(W, c0):
        return [(lambda dst: dst.rearrange("p (k c) -> p k c", k=8),
                 W[:, c0:c0 + 512].rearrange("(k p) c -> p k c", p=128))]

    pidx = {}
    for l in range(L):
        for nm, W in (("wk", wk_d), ("wv", wv_d)):
            for pc in range(2):
                pidx[(nm, l, pc)] = len(pieces)
                pieces.append(colpiece(W[l], pc * 512))
    for l in range(L):
        for pc in range(3):
            pidx[("w_in", l, pc)] = len(pieces)
            pieces.append(colpiece(w_in_d[l], pc * 512))
        for nm, W in (("w_out", w_out_d), ("wq", wq_d), ("wo", wo_d)):
            for pc in range(2):
                pidx[(nm, l, pc)] = len(pieces)
                pieces.append(colpiece(W[l], pc * 512))
        for i in range(11):
            pidx[("w_up", l, i)] = len(pieces)
            pieces.append([
                (lambda dst: dst.rearrange("p (k c) -> p k c", k=8)[:, :, 0:256],
                 w_up_d[l][:, 256 * i:256 * i + 256].rearrange("(k p) c -> p k c", p=128)),
                (lambda dst: dst.rearrange("p (k c) -> p k c", k=8)[:, :, 256:512],
                 w_up_d[l][:, DFF + 256 * i:DFF + 256 * i + 256].rearrange("(k p) c -> p k c", p=128)),
            ])
        for m in range(8):
            pidx[("w_down", l, m)] = len(pieces)
            pieces.append([(lambda dst: dst[:, 0:2816].rearrange("p (k c) -> p k c", k=22),
                            w_down_d[l][:, m * 128:(m + 1) * 128].rearrange("(k p) c -> p k c", p=128))])
    NP = len(pieces)
    plen = {pi: PIECE for pi in range(NP)}
    for l in range(L):
        for m in range(8):
            plen[pidx[("w_down", l, m)]] = 22 * 128
    wscr = nc.dram_tensor("wscr", [NP, 128, PIECE], BF16, kind="Internal").ap()

    PE, ACT, DVE, POOL, SP = Eng("pe"), Eng("act"), Eng("dve"), Eng("pool"), Eng("sp")
    engs = [PE, ACT, DVE, POOL, SP]

    with contextlib.ExitStack() as es:
        def sb(name, shape, dt):
            return es.enter_context(nc.sbuf_tensor(name, list(shape), dt))

        for e in engs:
            e.sem = es.enter_context(nc.semaphore("s_" + e.name))

        def dsem(name):
            return DmaSem(es.enter_context(nc.semaphore(name)))

        h4 = sb("h4", (128, NS, KC, SUB), F32)
        xn = sb("xn", (128, NS, KC, SUB), BF16)
        arena = [sb("arena%d" % s, (128, 16896), BF16) for s in range(NS)]
        shr = sb("shr", (128, 7296), BF16)
        sqr = sb("sqr", (128, 4, SUB), BF16)
        rstd = sb("rstd", (128, NS, SUB), F32)
        stg = sb("stg", (128, 2, D), F32)
        KT = sb("KT", (128, L, KC, NMEM), BF16)
        Vt = sb("Vt", (128, L, 2, D), BF16)
        cst = sb("cst", (128, NC_), F32)
        poolw = sb("poolw", (128, L * 4, 128), BF16)
        wmT = sb("wmT", (128, L * 4, 128), BF16)
        ident = sb("ident_sb", (128, 128), F32)
        mask = sb("mask_sb", (128, 128), F32)
        onesD = sb("onesD", (128, 128), BF16)
        ones1 = sb("ones1", (128, 128), BF16)
        hs = sb("hs", (128, 2, L, 22, 2, 2), F32)
        phalo = sb("phalo", (128, L, 4, 16), F32)
        small = sb("small", (128, 256), F32)
        ring = sb("ring", (128, NSLOT, PIECE), BF16)
        banks = [es.enter_context(nc.psum_tensor("bank%d" % i, [128, SUB], F32)) for i in range(8)]
        bankb = [Buf("bank%d" % i) for i in range(8)]
        bank_rr = [0]

        def next_bank():
            i = bank_rr[0]
            bank_rr[0] = (i + 1) % 8
            return banks[i], bankb[i]

        hb = [[Buf("h%d_%d" % (s, k)) for k in range(KC)] for s in range(NS)]
        xnb = [[Buf("xn%d_%d" % (s, k)) for k in range(KC)] for s in range(NS)]
        sqb = [Buf("sq%d" % i) for i in range(4)]
        rstdb = [Buf("rstd%d" % s) for s in range(NS)]
        stgb = [Buf("stg%d" % i) for i in range(2)]
        stghb = [[Buf("stg%d_h%d" % (i, hf)) for hf in range(2)] for i in range(2)]
        stg_st = [dsem("stg_st%d" % i) for i in range(4)]
        KTb = [Buf("KT%d" % l) for l in range(L)]
        Vb = [Buf("V%d" % l) for l in range(L)]
        cstb, poolwb, wmTb, identb, maskb, onesb = Buf("cst"), Buf("poolw"), Buf("wmT"), Buf("ident"), Buf("mask"), Buf("ones")
        hsb = [[[Buf("hs%d_%d_%d" % (q, l, c)) for c in range(NFC)] for l in range(L)] for q in range(2)]
        phb = [Buf("phalo%d" % l) for l in range(L)]
        ringb = [Buf("ring%d" % i) for i in range(NSLOT)]
        ring_sem = [dsem("ring%d" % i) for i in range(NSLOT)]
        csem = dsem("csem")
        sgwsem = dsem("sgwsem")
        memsem = dsem("memsem")
        pwsem = dsem("pwsem")
        conv_sem = [dsem("conv%d" % i) for i in range(NCS)]
        convb = [Buf("convsem%d" % i) for i in range(NCS)]
        scrb = [Buf("scr%d" % i) for i in range(NP)]

        def aview(s, lo, hi, dt=BF16):
            v = arena[s][:, lo // 2:hi // 2]
            return v.bitcast(F32) if dt == F32 else v

        def abuf(name, s, lo, hi):
            return Buf("%s_%d" % (name, s), arena=("arena", s), lo=lo, hi=hi)

        actv = [aview(s, 0, 22528).rearrange("p (k c) -> p k c", k=22) for s in range(NS)]
        actb = [[abuf("act%d" % k, s, k * 1024, (k + 1) * 1024) for k in range(22)] for s in range(NS)]
        cav = [[aview(s, 22528 + i * 2048, 22528 + (i + 1) * 2048, F32) for i in range(4)] for s in range(NS)]
        cab = [[abuf("ca%d" % i, s, 22528 + i * 2048, 22528 + (i + 1) * 2048) for i in range(4)] for s in range(NS)]
        capv = [[aview(s, 22528 + r * 4096, 22528 + (r + 1) * 4096, F32).rearrange("p (g c) -> p g c", g=2) for r in range(2)] for s in range(NS)]
        sgv = [aview(s, 30720, 32768, F32) for s in range(NS)]
        sgb = [abuf("sg", s, 30720, 32768) for s in range(NS)]
        a8v = [aview(s, 0, 8192).rearrange("p (k c) -> p k c", k=8) for s in range(NS)]
        a8b = [[abuf("a8_%d" % k, s, k * 1024, (k + 1) * 1024) for k in range(8)] for s in range(NS)]
        pv = [aview(s, 8192, 8192 + 8448, F32).rearrange("p (g c) -> p g c", g=4) for s in range(NS)]
        pb = [[abuf("p%d" % g, s, 8192 + g * 2112, 8192 + (g + 1) * 2112) for g in range(4)] for s in range(NS)]
        uv = [aview(s, 16640, 20736).rearrange("p (k c) -> p k c", k=4) for s in range(NS)]
        ub = [[abuf("u%d" % k, s, 16640 + k * 1024, 16640 + (k + 1) * 1024) for k in range(4)] for s in range(NS)]
        ddv = [aview(s, 20736, 24832).rearrange("p (k c) -> p k c", k=4) for s in range(NS)]
        ddb = [[abuf("d%d" % k, s, 20736 + k * 1024, 20736 + (k + 1) * 1024) for k in range(4)] for s in range(NS)]
        vnv = [aview(s, 24832, 28928).rearrange("p (k c) -> p k c", k=4) for s in range(NS)]
        vnb = [[abuf("vn%d" % k, s, 24832 + k * 1024, 24832 + (k + 1) * 1024) for k in range(4)] for s in range(NS)]
        xbv = [aview(s, 28928, 33024).rearrange("p (k c) -> p k c", k=4) for s in range(NS)]
        xbb = [[abuf("xb%d" % k, s, 28928 + k * 1024, 28928 + (k + 1) * 1024) for k in range(4)] for s in range(NS)]

        def sview(lo, hi, dt=BF16):
            v = shr[:, lo // 2:hi // 2]
            return v.bitcast(F32) if dt == F32 else v

        def sbuf_(name, lo, hi):
            return Buf(name, arena=("shr", 0), lo=lo, hi=hi)

        ptv = [sview(i * 2112, (i + 1) * 2112, F32) for i in range(2)]
        ptb = [sbuf_("pt%d" % i, i * 2112, (i + 1) * 2112) for i in range(2)]
        vfv = [sview(4224 + i * 2048, 4224 + (i + 1) * 2048, F32) for i in range(3)]
        vfb = [sbuf_("vf%d" % i, 4224 + i * 2048, 4224 + (i + 1) * 2048) for i in range(3)]
        ptv2 = [sview(10368 + i * 2112, 10368 + (i + 1) * 2112, F32) for i in range(2)]
        ptb2 = [sbuf_("pt2_%d" % i, 10368 + i * 2112, 10368 + (i + 1) * 2112) for i in range(2)]
        sgtv = [sview(10368 + i * 2048, 12416 + i * 2048, F32) for i in range(2)]
        sgtb = [sbuf_("sgt%d" % i, 10368 + i * 2048, 12416 + i * 2048) for i in range(2)]
        PTv = [sview(i * 1024, (i + 1) * 1024) for i in range(6)]
        PTb = [sbuf_("PT%d" % i, i * 1024, (i + 1) * 1024) for i in range(6)]
        rdv = [sview(6144 + i * 2048, 6144 + (i + 1) * 2048, F32) for i in range(2)]
        rdb = [sbuf_("rd%d" % i, 6144 + i * 2048, 6144 + (i + 1) * 2048) for i in range(2)]
        NXS = 3
        xsv = sview(0, NXS * 4096, F32).rearrange("p (i f) -> p i f", i=NXS)
        xsb = [sbuf_("xs%d" % i, i * 4096, (i + 1) * 4096) for i in range(NXS)]
        xs_ld = [dsem("xs_ld%d" % i) for i in range(NXS)]
        smallb = [Buf("small%d" % i) for i in range(16)]

        C = lambda off, n=1: cst[:, off:off + n]

        seq = []
        for ti in range(NT):
            for l in range(L):
                seq += [pidx[("w_in", l, 0)], pidx[("w_in", l, 2)], pidx[("w_in", l, 1)]]
                for nm in ("w_out", "wq") + (("wk", "wv") if ti == 0 else ()) + ("wo",):
                    seq += [pidx[(nm, l, 0)], pidx[(nm, l, 1)]]
                seq += [pidx[("w_up", l, i)] for i in range(11)]
                seq += [pidx[("w_down", l, m)] for m in range(8)]
        ws = {"load": 0, "use": 0}

        def ws_load_next():
            n = ws["load"]
            if n >= len(seq):
                return
            ws["load"] = n + 1
            s = n % NSLOT
            pi = seq[n]
            conv_issue_upto(n + 1 + CONV_AHEAD)
            n_el = plen[pi]
            emit_dma(SP, [lambda e, s=s, pi=pi, n_el=n_el: e.dma_start(out=ring[:, s, 0:n_el], in_=wscr[pi][:, 0:n_el])],
                     ring_sem[s], reads=[scrb[pi]], writes=[ringb[s]])

        def ws_acquire(expect, ahead=0):
            n = ws["use"] + ahead
            assert seq[n] == expect, (n, seq[n], expect)
            while ws["load"] <= n:
                ws_load_next()
            s = n % NSLOT
            return ring[:, s, :], ringb[s]

        def ws_release():
            ws["use"] += 1
            while ws["load"] < min(len(seq), ws["use"] + NSLOT):
                ws_load_next()

        def mm_group(pairs, reads, bank=None):
            if bank is None:
                bank = next_bank()
            bt, bb = bank
            n = len(pairs)

            def fn(e):
                for i, (l_, r_) in enumerate(pairs):
                    ins = e.matmul(bt[:, 0:r_.shape[-1]] if r_.shape[-1] != SUB else bt[:], l_, r_, start=(i == 0), stop=(i == n - 1))
                return ins
            emit(PE, fn, reads=reads, writes=[bb], nmm=n)
            return bt, bb

        small_rr = {"ln": [0, 0, 4], "fix": [0, 4, 8], "misc": [0, 12, 4]}

        def next_small(kind="misc"):
            st = small_rr[kind]
            i = st[1] + st[0]
            st[0] = (st[0] + 1) % st[2]
            return small[:, i * 16:(i + 1) * 16], smallb[i]

        emit_dma(SP, [lambda e: e.dma_start(out=cst[:], in_=consts_d),
                      lambda e: e.dma_start(out=ident[:], in_=ident_d),
                      lambda e: e.dma_start(out=mask[:], in_=mask_d)], csem, writes=[cstb, identb, maskb])
        emit(POOL, lambda e: e.memset(onesD[:], 1.0 / D), writes=[onesb])
        emit(POOL, lambda e: e.memset(ones1[:], 1.0), writes=[onesb])
        emit(POOL, lambda e: e.memset(hs[:].rearrange("p a l j g t -> p (a l j g t)"), 0.0),
             writes=[b for q in range(2) for l in range(L) for b in hsb[q][l]])
        emit(POOL, lambda e: e.memset(phalo[:].rearrange("p l g c -> p (l g c)"), 0.0), writes=phb)
        emit_dma(POOL, [lambda e: e.dma_start(out=poolw[:], in_=pool_w_d.rearrange("l g c d -> c (l g) d"))],
                 pwsem, writes=[poolwb])
        order = []
        for pi in seq:
            if pi not in order:
                order.append(pi)
        conv_next = [0]

        def conv_issue_upto(k):
            while conv_next[0] < min(k, len(order)):
                n = conv_next[0]
                pi = order[n]
                fns = [(lambda e, dfn=dfn, src=src, pi=pi: e.dma_start(out=dfn(wscr[pi]), in_=src)) for dfn, src in pieces[pi]]
                emit_dma(POOL, fns, conv_sem[n % NCS], writes=[scrb[pi], convb[n % NCS]])
                conv_next[0] = n + 1

        sgw_v = aview(0, 8192, 8192 + L * 2048, F32).rearrange("p (a s) -> p a s", a=L * 4)
        sgw_b = abuf("sgw", 0, 8192, 8192 + L * 2048)
        emit_dma(SP, [lambda e: e.dma_start(out=sgw_v, in_=sgu_w_d.rearrange("l h t s -> t (l h) s"))], sgwsem, writes=[sgw_b])
        for a0 in range(0, L * 4, 4):
            bt, bb = next_bank()

            def fn(e, a0=a0, bt=bt):
                for j in range(4):
                    ins = e.transpose(bt[:, j * 128:(j + 1) * 128], sgw_v[:, a0 + j, :], ident[:])
                return ins
            emit(PE, fn, reads=[sgw_b, identb], writes=[bb], nmm=4)
            emit(DVE, lambda e, a0=a0, bt=bt: e.tensor_tensor(
                out=wmT[:, a0:a0 + 4, :], in0=bt[:].rearrange("p (a t) -> p a t", a=4),
                in1=mask[:, None, :].broadcast_to([128, 4, 128]), op=ALU.mult), reads=[bb, maskb], writes=[wmTb])

        memv = stg
        emit_dma(SP, [lambda e: e.dma_start(out=stg[:], in_=mem_d.rearrange("(c p) f -> p c f", p=128))], memsem, writes=[stgb[0], stgb[1]])
        for c in range(2):
            sv, sbf = next_small()
            emit(DVE, lambda e, sv=sv, c=c: e.bn_stats(out=sv[:, 0:6], in_=memv[:, c, 0:512]), reads=[stgb[c]], writes=[sbf])
            emit(DVE, lambda e, sv=sv, c=c: e.bn_stats(out=sv[:, 6:12], in_=memv[:, c, 512:1024]), reads=[stgb[c]], writes=[sbf])
            emit(DVE, lambda e, sv=sv: e.bn_aggr(out=sv[:, 12:14], in_=sv[:, 0:12]), reads=[sbf], writes=[sbf])
            emit(DVE, lambda e, sv=sv: e.scalar_tensor_tensor(out=sv[:, 14:15], in0=sv[:, 12:13], scalar=sv[:, 12:13], in1=sv[:, 13:14],
                                                             op0=ALU.mult, op1=ALU.add), reads=[sbf], writes=[sbf])
            emit(ACT, lambda e, sv=sv: e.activation(out=sv[:, 15:16], in_=sv[:, 14:15], func=AF.Sqrt, bias=EPS), reads=[sbf], writes=[sbf])
            emit(DVE, lambda e, sv=sv: e.reciprocal(out=sv[:, 15:16], in_=sv[:, 15:16]), reads=[sbf], writes=[sbf])
            emit(DVE, lambda e, sv=sv, c=c: e.tensor_scalar(out=memv[:, c, :], in0=memv[:, c, :], scalar1=sv[:, 15:16], scalar2=None, op0=ALU.mult),
                 reads=[sbf, stgb[c]], writes=[stgb[c]])
        def kv_compute(l):
            mT_v = aview(0, 8192, 12288).rearrange("p (k m) -> p k m", k=8)
            mT_b = abuf("memnT%d" % l, 0, 8192, 12288)
            for c in range(2):
                for k0 in (0, 4):
                    bt, bb = next_bank()

                    def fn(e, c=c, k0=k0, bt=bt):
                        for j in range(4):
                            ins = e.transpose(bt[:, j * 128:(j + 1) * 128], memv[:, c, (k0 + j) * 128:(k0 + j + 1) * 128], ident[:])
                        return ins
                    emit(PE, fn, reads=[stgb[c], identb], writes=[bb], nmm=4)
                    emit(DVE, lambda e, c=c, k0=k0, bt=bt, l=l, mT_v=mT_v: e.tensor_tensor(
                        out=mT_v[:, k0:k0 + 4, c * 128:(c + 1) * 128], in0=bt[:].rearrange("p (k m) -> p k m", k=4),
                        in1=cst[:, l * LC + O_GMEM + k0:l * LC + O_GMEM + k0 + 4].unsqueeze(2).broadcast_to([128, 4, 128]), op=ALU.mult),
                        reads=[bb, cstb], writes=[mT_b])
            for pc in range(2):
                slot, slb = ws_acquire(pidx[("wk", l, pc)])
                s3 = slot.rearrange("p (k c) -> p k c", k=8)
                for mm in range(4):
                    dch = pc * 4 + mm
                    bt, bb = mm_group([(s3[:, k, mm * 128:(mm + 1) * 128], mT_v[:, k, :]) for k in range(8)], reads=[slb, mT_b])
                    emit(ACT, lambda e, bt=bt, l=l, dch=dch: e.activation(out=KT[:, l, dch, :], in_=bt[:, 0:NMEM], func=AF.Copy),
                         reads=[bb], writes=[KTb[l]])
                ws_release()
            for pc in range(2):
                slot, slb = ws_acquire(pidx[("wv", l, pc)])
                s3 = slot.rearrange("p (k c) -> p k c", k=8)
                for mc in range(2):
                    bt, bb = mm_group([(mT_v[:, k, mc * 128:(mc + 1) * 128], s3[:, k, :]) for k in range(8)], reads=[slb, mT_b])
                    emit(ACT, lambda e, bt=bt, l=l, mc=mc, pc=pc: e.activation(out=Vt[:, l, mc, pc * 512:(pc + 1) * 512], in_=bt[:], func=AF.Copy),
                         reads=[bb], writes=[Vb[l]])
                ws_release()

        sq_rr = [0]

        def norm_stats(s):
            bank = next_bank()
            bt, bb = bank
            for k in range(KC):
                i = sq_rr[0]
                sq_rr[0] = (i + 1) % 4
                emit(ACT, lambda e, i=i, k=k: e.activation(out=sqr[:, i, :], in_=h4[:, s, k, :], func=AF.Square),
                     reads=[hb[s][k]], writes=[sqb[i]])
                emit(PE, lambda e, i=i, k=k: e.matmul(bt[:], onesD[:], sqr[:, i, :], start=(k == 0), stop=(k == KC - 1)),
                     reads=[sqb[i], onesb], writes=[bb], nmm=1)
            emit(ACT, lambda e: e.activation(out=rstd[:, s, :], in_=bt[:], func=AF.Ln, bias=C(O_EPS)), reads=[bb, cstb], writes=[rstdb[s]])
            emit(ACT, lambda e: e.activation(out=rstd[:, s, :], in_=rstd[:, s, :], func=AF.Exp, scale=-0.5), reads=[rstdb[s]], writes=[rstdb[s]])

        xn_pend = {}

        def xn_emit(s, n):
            st = xn_pend.get(s)
            while st is not None and n > 0 and st[1] < KC:
                goff, k = st
                emit(DVE, lambda e, k=k, goff=goff: e.scalar_tensor_tensor(out=xn[:, s, k, :], in0=h4[:, s, k, :], scalar=C(goff + k),
                                                                           in1=rstd[:, s, :], op0=ALU.mult, op1=ALU.mult),
                     reads=[hb[s][k], rstdb[s], cstb], writes=[xnb[s][k]])
                st[1] += 1
                n -= 1
            if st is not None and st[1] >= KC:
                del xn_pend[s]

        def xn_flush(s):
            xn_emit(s, KC)

        def norm_begin(s, goff):
            norm_stats(s)
            xn_pend[s] = [goff, 0]

        def load_x_dma(ti, tcg):
            i = tcg % NXS
            r0 = ti * T + tcg * 128
            emit_dma(SP, [lambda e, i=i, r0=r0: e.dma_start(out=xsv[:, i, :], in_=x_d[r0:r0 + 128, :])], xs_ld[i], writes=[xsb[i]])

        def load_x_chunk(ti, tcg, after):
            s, tc = divmod(tcg, SUB // 128)
            i = tcg % NXS
            b0, b1 = next_bank(), next_bank()

            def fn(e, i=i, b0=b0, b1=b1):
                for kc in range(KC):
                    bt = (b0 if kc < 4 else b1)[0]
                    ins = e.transpose(bt[:, (kc % 4) * 128:(kc % 4 + 1) * 128], xsv[:, i, kc * 128:(kc + 1) * 128], ident[:])
                return ins
            emit(PE, fn, reads=[xsb[i], identb], writes=[b0[1], b1[1]], nmm=8)
            if tcg + NXS < T // 128:
                load_x_dma(ti, tcg + NXS)
            emit(ACT, lambda e: e.activation(out=h4[:, s, 0:4, tc * 128:(tc + 1) * 128],
                                             in_=b0[0][:].rearrange("p (k t) -> p k t", k=4), func=AF.Copy),
                 reads=[b0[1]], writes=hb[s][0:4])
            emit(DVE, lambda e: e.tensor_copy(out=h4[:, s, 4:8, tc * 128:(tc + 1) * 128],
                                              in_=b1[0][:].rearrange("p (k t) -> p k t", k=4)),
                 reads=[b1[1]], writes=hb[s][4:8])
            if s == 1:
                xn_emit(0, 2)
            if tc == SUB // 128 - 1:
                after(s)

        vf_rr = [0]

        def mixer_p(s, l, slot, slb, first, between=None):
            s3 = slot.rearrange("p (k c) -> p k c", k=8)
            emit(POOL, lambda e: e.tensor_copy(out=pv[s][:, :, 0:16], in_=phalo[:, l, :, :]), reads=[phb[l]], writes=pb[s])
            for m in range(4):
                bt, bb = mm_group([(s3[:, k, m * 128:(m + 1) * 128], xn[:, s, k, :]) for k in range(KC)], reads=[slb] + xnb[s])
                emit(ACT, lambda e, bt=bt, m=m: e.activation(out=pv[s][:, m, 16:528], in_=bt[:], func=AF.Identity, scale=1.0 / WINS[m], bias=0.0),
                     reads=[bb], writes=[pb[s][m]])
                emit(ACT, lambda e, bt=bt, m=m: e.activation(out=xbv[s][:, m, :], in_=bt[:], func=AF.Copy), reads=[bb], writes=[xbb[s][m]])
                if between is not None:
                    between()
            emit(POOL, lambda e: e.tensor_copy(out=phalo[:, l, :, :], in_=pv[s][:, :, 512:528]), reads=pb[s], writes=[phb[l]])

        pool_q = []

        def pool_drain(n=10 ** 6):
            while pool_q and n > 0:
                pool_q.pop(0)()
                n -= 1

        def pooling(s, l, first, POOL=POOL, tv=None, tb=None, queue=False):
            if queue:
                real_emit = emit

                def q_emit(E, fn, reads=(), writes=()):
                    pool_q.append(lambda: real_emit(E, fn, reads=reads, writes=writes))
                return _pooling(s, l, first, POOL, tv, tb, q_emit)
            return _pooling(s, l, first, POOL, tv, tb, emit)

        def _pooling(s, l, first, POOL, tv, tb, emit):
            tv = ptv if tv is None else tv
            tb = ptb if tb is None else tb
            for g, win in enumerate(WINS):
                X = pv[s][:, g, :]
                t1, t2 = tv[0], tv[1]
                steps = [(1, t1, tb[0]), (2, t2, tb[1]), (4, t1, tb[0]), (8, t2, tb[1])][:g + 1]
                src, srcb, lo = X, pb[s][g], 0
                for (sh, dst, dstb) in steps:
                    nlo = lo + sh
                    emit(POOL, lambda e, src=src, dst=dst, nlo=nlo, sh=sh: e.tensor_tensor(
                        out=dst[:, nlo:528], in0=src[:, nlo:528], in1=src[:, nlo - sh:528 - sh], op=ALU.add),
                        reads=[srcb], writes=[dstb])
                    src, srcb, lo = dst, dstb, nlo
                emit(POOL, lambda e, src=src, g=g: e.tensor_tensor(out=ddv[s][:, g, :], in0=src[:, 16:528], in1=xbv[s][:, g, :], op=ALU.subtract),
                     reads=[srcb, xbb[s][g]], writes=[ddb[s][g]])
                if first:
                    sv, sbf = next_small()
                    emit(POOL, lambda e, src=src, sv=sv, g=g: e.tensor_tensor(out=sv[:, 0:16], in0=src[:, 16:32], in1=C(O_INV + g * 16, 16), op=ALU.mult),
                         reads=[srcb, cstb], writes=[sbf])
                    emit(POOL, lambda e, sv=sv, g=g: e.tensor_tensor(out=ddv[s][:, g, 0:16], in0=sv[:, 0:16], in1=xbv[s][:, g, 0:16], op=ALU.subtract),
                         reads=[sbf, xbb[s][g]], writes=[ddb[s][g]])

        def mixer_v(s, l, slot, slb):
            s3 = slot.rearrange("p (k c) -> p k c", k=8)
            pend = []

            def ln_rstd(p_):
                sv, sbf, i, tc = p_
                emit(ACT, lambda e: e.activation(out=sv[:, 8:9], in_=sv[:, 7:8], func=AF.Sqrt, bias=C(O_EPS)), reads=[sbf, cstb], writes=[sbf])

            def normalize(p_):
                sv, sbf, i, tc = p_
                emit(DVE, lambda e: e.reciprocal(out=sv[:, 9:10], in_=sv[:, 8:9]), reads=[sbf], writes=[sbf])
                emit(DVE, lambda e: e.tensor_scalar(out=vnv[s][:, tc, :], in0=vfv[i], scalar1=sv[:, 6:7], scalar2=sv[:, 9:10],
                                                    op0=ALU.subtract, op1=ALU.mult), reads=[sbf, vfb[i]], writes=[vnb[s][tc]])
            for tc in range(4):
                bt, bb = mm_group([(xn[:, s, k, tc * 128:(tc + 1) * 128], s3[:, k, :]) for k in range(KC)], reads=[slb] + xnb[s])
                i = vf_rr[0]
                vf_rr[0] = (i + 1) % 3
                emit(ACT, lambda e, bt=bt, i=i: e.activation(out=vfv[i], in_=bt[:], func=AF.Gelu), reads=[bb], writes=[vfb[i]])
                sv, sbf = next_small("ln")
                emit(DVE, lambda e, sv=sv, i=i: e.bn_stats(out=sv[:, 0:6], in_=vfv[i]), reads=[vfb[i]], writes=[sbf])
                emit(DVE, lambda e, sv=sv: e.bn_aggr(out=sv[:, 6:8], in_=sv[:, 0:6]), reads=[sbf], writes=[sbf])
                pend.append((sv, sbf, i, tc))
                if tc == 2:
                    ln_rstd(pend[0])
                    ln_rstd(pend[1])
                    normalize(pend.pop(0))
                    normalize(pend.pop(0))
                pool_drain(2)
            for p_ in pend:
                ln_rstd(p_)
            while pend:
                normalize(pend.pop(0))

        def mixer_u(s, l, slot, slb):
            s3 = slot.rearrange("p (k c) -> p k c", k=8)
            for m in range(4):
                bt, bb = mm_group([(s3[:, k, m * 128:(m + 1) * 128], xn[:, s, k, :]) for k in range(KC)], reads=[slb] + xnb[s])
                emit(ACT, lambda e, bt=bt, m=m: e.activation(out=uv[s][:, m, :], in_=bt[:], func=AF.Gelu), reads=[bb], writes=[ub[s][m]])

        def mixer_mix(s, l):
            base = l * LC
            for g in range(4):
                bt, bb = mm_group([(poolw[:, l * 4 + g, :], ddv[s][:, g, :])], reads=[poolwb, ddb[s][g]])
                emit(ACT, lambda e, bt=bt, g=g: e.activation(out=a8v[s][:, g, :], in_=bt[:], func=AF.Identity, scale=C(base + O_PSC + g), bias=0.0),
                     reads=[bb, cstb], writes=[a8b[s][g]])
            for hh in range(4):
                bt, bb = next_bank()

                def fn(e, bt=bt, hh=hh):
                    for tc in range(4):
                        ins = e.matmul(bt[:, tc * 128:(tc + 1) * 128], vnv[s][:, tc, hh * 128:(hh + 1) * 128], wmT[:, l * 4 + hh, :], start=True, stop=True)
                    return ins
                emit(PE, fn, reads=vnb[s] + [wmTb], writes=[bb], nmm=4)
                gi = hh % 2
                emit(DVE, lambda e, bt=bt, hh=hh, gi=gi: e.scalar_tensor_tensor(
                    out=sgtv[gi].rearrange("p (r t) -> p r t", r=4), in0=bt[:].rearrange("p (r t) -> p r t", r=4), scalar=C(base + O_SGG + hh),
                    in1=cst[:, base + O_BBC + hh * 128:base + O_BBC + (hh + 1) * 128][:, None, :].broadcast_to([128, 4, 128]),
                    op0=ALU.mult, op1=ALU.add), reads=[bb, cstb], writes=[sgtb[gi]])
                emit(POOL, lambda e, hh=hh, gi=gi: e.tensor_tensor(out=a8v[s][:, 4 + hh, :], in0=sgtv[gi], in1=uv[s][:, hh, :], op=ALU.mult),
                     reads=[sgtb[gi], ub[s][hh]], writes=[a8b[s][4 + hh]])

        def proj_resid(s, pc, slot, slb, between=None):
            s3 = slot.rearrange("p (k c) -> p k c", k=8)
            for mm in range(4):
                m = pc * 4 + mm
                bt, bb = mm_group([(s3[:, k, mm * 128:(mm + 1) * 128], a8v[s][:, k, :]) for k in range(KC)], reads=[slb] + a8b[s])
                emit(DVE, lambda e, bt=bt, m=m: e.tensor_tensor(out=h4[:, s, m, :], in0=bt[:], in1=h4[:, s, m, :], op=ALU.add),
                     reads=[bb, hb[s][m]], writes=[hb[s][m]])
                if between is not None:
                    between()

        def proj_q(s, pc, slot, slb, between=None):
            s3 = slot.rearrange("p (k c) -> p k c", k=8)
            for mm in range(4):
                m = pc * 4 + mm
                bt, bb = mm_group([(s3[:, k, mm * 128:(mm + 1) * 128], xn[:, s, k, :]) for k in range(KC)], reads=[slb] + xnb[s])
                emit(ACT, lambda e, bt=bt, m=m: e.activation(out=a8v[s][:, m, :], in_=bt[:], func=AF.Copy), reads=[bb], writes=[a8b[s][m]])
                if between is not None:
                    between()

        def attention_all(l):
            units = [(s, hh) for hh in range(4) for s in range(NS)]

            def scores(u):
                s, hh = units[u]
                r = u % 3
                for mc in range(2):
                    bt, bb = mm_group([(KT[:, l, 2 * hh + dc, mc * 128:(mc + 1) * 128], a8v[s][:, 2 * hh + dc, :]) for dc in range(2)],
                                      reads=[KTb[l], a8b[s][2 * hh], a8b[s][2 * hh + 1]])
                    emit(ACT, lambda e, bt=bt, r=r, mc=mc: e.activation(out=PTv[2 * r + mc], in_=bt[:], func=AF.Exp, scale=1.0 / 16.0),
                         reads=[bb], writes=[PTb[2 * r + mc]])

            def rest(u):
                s, hh = units[u]
                r = u % 3
                rr = u % 2
                bt, bb = mm_group([(ones1[:], PTv[2 * r + mc]) for mc in range(2)], reads=[onesb, PTb[2 * r], PTb[2 * r + 1]])
                emit(ACT, lambda e, bt=bt, rr=rr: e.activation(out=rdv[rr], in_=bt[:], func=AF.Ln), reads=[bb], writes=[rdb[rr]])
                emit(ACT, lambda e, rr=rr: e.activation(out=rdv[rr], in_=rdv[rr], func=AF.Exp, scale=-1.0), reads=[rdb[rr]], writes=[rdb[rr]])
                for dc in range(2):
                    dch = 2 * hh + dc
                    bt, bb = mm_group([(Vt[:, l, mc, dch * 128:(dch + 1) * 128], PTv[2 * r + mc]) for mc in range(2)],
                                      reads=[Vb[l], PTb[2 * r], PTb[2 * r + 1]])
                    emit(DVE, lambda e, bt=bt, rr=rr, dch=dch, s=s: e.tensor_tensor(out=a8v[s][:, dch, :], in0=bt[:], in1=rdv[rr], op=ALU.mult),
                         reads=[bb, rdb[rr]], writes=[a8b[s][dch]])

            n = len(units)
            scores(0)
            scores(1)
            for u in range(n):
                if u + 2 < n:
                    scores(u + 2)
                rest(u)

        ca_rr = [0, 0]
        hs_par = {}

        ffn_pend = []

        def ffn_flush():
            while ffn_pend:
                ffn_pend.pop(0)()

        def ffn_up(s, l, i, slot, slb, between=None):
            base = l * LC
            s3 = slot.rearrange("p (k c) -> p k c", k=8)
            for j in range(2):
                jj = 2 * i + j
                r = ca_rr[s]
                ca_rr[s] ^= 1
                q = hs_par.get((l, jj), 0)
                hs_par[(l, jj)] = q ^ 1
                res = []
                for which, coff in ((0, j * 128), (1, 256 + j * 128)):
                    cf = jj + 22 * which
                    cc = 2 * jj + which
                    bt, bb = mm_group([(s3[:, k, coff:coff + 128], xn[:, s, k, :]) for k in range(KC)], reads=[slb] + xnb[s])
                    av, ab = cav[s][2 * r + which], cab[s][2 * r + which]
                    emit(ACT, lambda e, bt=bt, av=av, cc=cc: e.activation(out=av, in_=bt[:], func=AF.Identity, scale=C(base + O_CW2 + cc), bias=C(base + O_CB + cc)),
                         reads=[bb, cstb], writes=[ab])
                    emit(ACT, lambda e, bt=bt, jj=jj, which=which, q=q: e.activation(out=hs[:, q ^ 1, l, jj, which, :], in_=bt[:, 510:512], func=AF.Copy),
                         reads=[bb], writes=[hsb[q ^ 1][l][cf]])
                    emit(DVE, lambda e, bt=bt, av=av, cc=cc: e.scalar_tensor_tensor(out=av[:, 1:512], in0=bt[:, 0:511], scalar=C(base + O_CW1 + cc),
                                                                                 in1=av[:, 1:512], op0=ALU.mult, op1=ALU.add),
                         reads=[bb, ab, cstb], writes=[ab])
                    emit(DVE, lambda e, bt=bt, av=av, cc=cc: e.scalar_tensor_tensor(out=av[:, 2:512], in0=bt[:, 0:510], scalar=C(base + O_CW0 + cc),
                                                                                 in1=av[:, 2:512], op0=ALU.mult, op1=ALU.add),
                         reads=[bb, ab, cstb], writes=[ab])
                    res.append((av, ab))
                (ag, agb), (avv, avb) = res
                sv, sbf = next_small("fix")
                hsp = hs[:, q, l, jj, :, :]
                hsrd = [hsb[q][l][jj], hsb[q][l][22 + jj]]
                cw0p = cst[:, base + O_CW0 + 2 * jj:base + O_CW0 + 2 * jj + 2].unsqueeze(2)
                cw1p = cst[:, base + O_CW1 + 2 * jj:base + O_CW1 + 2 * jj + 2].unsqueeze(2)
                ap_ = capv[s][r]
                emit(POOL, lambda e, sv=sv, hsp=hsp, cw0p=cw0p: e.tensor_tensor(out=sv[:, 0:4].rearrange("p (g t) -> p g t", g=2), in0=hsp,
                                                                               in1=cw0p.broadcast_to([128, 2, 2]), op=ALU.mult),
                     reads=hsrd + [cstb], writes=[sbf])
                emit(POOL, lambda e, sv=sv, hsp=hsp, cw1p=cw1p: e.tensor_tensor(out=sv[:, 4:6].rearrange("p (g t) -> p g t", g=2), in0=hsp[:, :, 1:2],
                                                                               in1=cw1p, op=ALU.mult),
                     reads=hsrd + [cstb], writes=[sbf])
                emit(POOL, lambda e, sv=sv, ap_=ap_: e.tensor_tensor(out=ap_[:, :, 0:2], in0=ap_[:, :, 0:2], in1=sv[:, 0:4].rearrange("p (g t) -> p g t", g=2), op=ALU.add),
                     reads=[sbf, agb, avb], writes=[agb, avb])
                emit(POOL, lambda e, sv=sv, ap_=ap_: e.tensor_tensor(out=ap_[:, :, 0:1], in0=ap_[:, :, 0:1], in1=sv[:, 4:6].rearrange("p (g t) -> p g t", g=2), op=ALU.add),
                     reads=[sbf, agb, avb], writes=[agb, avb])

                def finish(ag=ag, agb=agb, avv=avv, avb=avb, jj=jj, s=s):
                    emit(ACT, lambda e: e.activation(out=sgv[s], in_=ag, func=AF.Silu), reads=[agb], writes=[sgb[s]])
                    emit(POOL, lambda e: e.tensor_tensor(out=actv[s][:, jj, :], in0=sgv[s], in1=avv, op=ALU.mult),
                         reads=[sgb[s], avb], writes=[actb[s][jj]])
                ffn_flush()
                ffn_pend.append(finish)
                if between is not None:
                    between()
                    between()

        def ffn_down(s, m, slot, slb, pre_evac=None, k0=0, k1=22):
            s3 = slot[:, 0:2816].rearrange("p (k c) -> p k c", k=22)
            bt, bb = mm_group([(s3[:, k, :], actv[s][:, k, :]) for k in range(k0, k1)], reads=[slb] + actb[s][k0:k1])
            if pre_evac is not None:
                pre_evac()
            emit(DVE, lambda e, bt=bt: e.tensor_tensor(out=h4[:, s, m, :], in0=bt[:], in1=h4[:, s, m, :], op=ALU.add),
                 reads=[bb, hb[s][m]], writes=[hb[s][m]])

        out_i = [0]

        def final_scale(s):
            for k in range(KC):
                emit(DVE, lambda e, k=k: e.scalar_tensor_tensor(out=h4[:, s, k, :], in0=h4[:, s, k, :], scalar=C(O_GFIN + k),
                                                              in1=rstd[:, s, :], op0=ALU.mult, op1=ALU.mult),
                     reads=[hb[s][k], rstdb[s], cstb], writes=[hb[s][k]])

        def final_store_chunk(s, ti, tc):
            i = out_i[0]
            out_i[0] ^= 1
            b0, b1 = next_bank(), next_bank()

            def fn(e, b0=b0, b1=b1, tc=tc):
                for kc in range(KC):
                    bt = (b0 if kc < 4 else b1)[0]
                    ins = e.transpose(bt[:, (kc % 4) * 128:(kc % 4 + 1) * 128], h4[:, s, kc, tc * 128:(tc + 1) * 128], ident[:])
                return ins
            emit(PE, fn, reads=hb[s] + [identb], writes=[b0[1], b1[1]], nmm=8)
            r0 = ti * T + s * SUB + tc * 128
            emit(ACT, lambda e: e.activation(out=stg[:, i, 0:512], in_=b0[0][:], func=AF.Copy), reads=[b0[1]], writes=[stgb[i], stghb[i][0]])
            emit_dma(ACT, [lambda e: e.dma_start(out=out_d[r0:r0 + 128, 0:512], in_=stg[:, i, 0:512])], stg_st[2 * i], reads=[stghb[i][0]])
            emit(DVE, lambda e: e.tensor_copy(out=stg[:, i, 512:1024], in_=b1[0][:]), reads=[b1[1]], writes=[stgb[i], stghb[i][1]])
            emit_dma(SP, [lambda e: e.dma_start(out=out_d[r0:r0 + 128, 512:1024], in_=stg[:, i, 512:1024])], stg_st[2 * i + 1], reads=[stghb[i][1]])

        def piece_step(key, fn, after=None):
            STAGE[0] = "%s%s" % (key[0], key[2] if key[0] in ("w_in",) else "")
            slot, slb = ws_acquire(pidx[key])
            for s in range(NS):
                fn(s, slot, slb)
                if after is not None:
                    after(s)
            ws_release()

        def xn_flush_0():
            xn_flush(0)

        def final_norm(s):
            norm_stats(s)
            final_scale(s)

        def norm_piece(key, fn, goff):
            STAGE[0] = key[0]
            slot, slb = ws_acquire(pidx[key])
            fn(0, slot, slb, None)
            norm_begin(0, goff)
            fn(1, slot, slb, lambda: xn_emit(0, 2))
            xn_flush(0)
            norm_begin(1, goff)
            xn_flush(1)
            ws_release()

        def first_piece(key, fn):
            STAGE[0] = "%s%s" % (key[0], key[2] if key[0] in ("w_in",) else "")
            slot, slb = ws_acquire(pidx[key])
            fn(0, slot, slb, lambda: xn_emit(1, 1))
            xn_flush(1)
            fn(1, slot, slb, None)
            ws_release()

        norm1_first = lambda s: norm_begin(s, O_GMIX)
        for ti in range(NT):
            STAGE[0] = "load_x"
            if ti == 0:
                for c_ in range(NXS):
                    load_x_dma(0, c_)
                for tcg in range(4):
                    load_x_chunk(0, tcg, norm1_first)
            for tcg in range(4, 8):
                load_x_chunk(ti, tcg, norm1_first)
            xn_flush(0)
            xn_flush(1)
            for l in range(L):
                base = l * LC
                first_piece(("w_in", l, 0), lambda s, slot, slb, bw: mixer_p(s, l, slot, slb, first=(ti == 0 and s == 0), between=bw))
                pooling(0, l, first=(ti == 0))
                pooling(1, l, first=False, POOL=DVE, tv=ptv2, tb=ptb2, queue=True)
                piece_step(("w_in", l, 2), lambda s, slot, slb: mixer_v(s, l, slot, slb))
                pool_drain()
                piece_step(("w_in", l, 1), lambda s, slot, slb: mixer_u(s, l, slot, slb))
                STAGE[0] = "mix"
                for s in range(NS):
                    mixer_mix(s, l)
                piece_step(("w_out", l, 0), lambda s, slot, slb: proj_resid(s, 0, slot, slb))
                norm_piece(("w_out", l, 1), lambda s, slot, slb, bw: proj_resid(s, 1, slot, slb, between=bw), base + O_GX)
                first_piece(("wq", l, 0), lambda s, slot, slb, bw: proj_q(s, 0, slot, slb, between=bw))
                piece_step(("wq", l, 1), lambda s, slot, slb: proj_q(s, 1, slot, slb))
                if ti == 0:
                    STAGE[0] = "kv"
                    kv_compute(l)
                STAGE[0] = "attn"
                attention_all(l)
                piece_step(("wo", l, 0), lambda s, slot, slb: proj_resid(s, 0, slot, slb))
                norm_piece(("wo", l, 1), lambda s, slot, slb, bw: proj_resid(s, 1, slot, slb, between=bw), base + O_GFFN)
                first_piece(("w_up", l, 0), lambda s, slot, slb, bw: ffn_up(s, l, 0, slot, slb, between=bw))
                for i in range(1, 11):
                    piece_step(("w_up", l, i), lambda s, slot, slb: ffn_up(s, l, i, slot, slb))
                ffn_flush()
                STAGE[0] = "w_down"
                sl0 = ws_acquire(pidx[("w_down", l, 0)])
                sl1 = ws_acquire(pidx[("w_down", l, 1)], ahead=1)
                for (k0, k1) in ((0, 16), (16, 22)):
                    for m, (slot, slb) in ((0, sl0), (1, sl1)):
                        for s in range(NS):
                            ffn_down(s, m, slot, slb, k0=k0, k1=k1)
                ws_release()
                ws_release()
                for m in range(2, 7):
                    piece_step(("w_down", l, m), lambda s, slot, slb: ffn_down(s, m, slot, slb))
                if l + 1 < L:
                    norm_piece(("w_down", l, 7), lambda s, slot, slb, bw: ffn_down(s, 7, slot, slb, pre_evac=(None if bw is None else xn_flush_0)),
                               (l + 1) * LC + O_GMIX)
                else:
                    if ti + 1 < NT:
                        for c_ in range(NXS):
                            load_x_dma(ti + 1, c_)
                    piece_step(("w_down", l, 7), lambda s, slot, slb: ffn_down(s, 7, slot, slb), after=final_norm)
            STAGE[0] = "final"
            for tc in range(4):
                final_store_chunk(0, ti, tc)
            for tc in range(4):
                final_store_chunk(1, ti, tc)
                if ti + 1 < NT:
                    load_x_chunk(ti + 1, tc, norm1_first)

        for i in range(4):
            if stg_st[i].val:
                ACT.ops.append(lambda e, i=i: e.wait_ge(stg_st[i].sem, stg_st[i].val))

        with nc.Block() as block:
            @block.tensor
            def _(e):
                for op in PE.ops:
                    op(e)

            @block.scalar
            def _(e):
                for op in ACT.ops:
                    op(e)

            @block.vector
            def _(e):
                for op in DVE.ops:
                    op(e)

            @block.gpsimd
            def _(e):
                for op in POOL.ops:
                    op(e)

            @block.sync
            def _(e):
                for op in SP.ops:
                    op(e)
    return nc


def make_consts(L, norm_mix_g, norm_xattn_g, norm_ffn_g, mem_norm_g, pool_scale, sgu_g, conv_w, conv_b, sgu_b, final_norm_g):
    NC_ = L * LC + 8 + 64 + 1
    c = np.zeros((128, NC_), np.float32)
    col = lambda v: np.asarray(v, np.float32).reshape(-1, 128).T
    for l in range(L):
        b = l * LC
        c[:, b + O_GMIX:b + O_GMIX + 8] = col(norm_mix_g[l])
        c[:, b + O_GX:b + O_GX + 8] = col(norm_xattn_g[l])
        c[:, b + O_GFFN:b + O_GFFN + 8] = col(norm_ffn_g[l])
        c[:, b + O_GMEM:b + O_GMEM + 8] = col(mem_norm_g[l])
        c[:, b + O_PSC:b + O_PSC + 4] = col(pool_scale[l])
        c[:, b + O_SGG:b + O_SGG + 4] = col(sgu_g[l])
        pair = lambda v: col(v).reshape(128, 2, 22).transpose(0, 2, 1).reshape(128, 44)
        c[:, b + O_CW0:b + O_CW0 + 44] = pair(conv_w[l, 0])
        c[:, b + O_CW1:b + O_CW1 + 44] = pair(conv_w[l, 1])
        c[:, b + O_CW2:b + O_CW2 + 44] = pair(conv_w[l, 2])
        c[:, b + O_CB:b + O_CB + 44] = pair(conv_b[l])
        c[:, b + O_BBC:b + O_BBC + 512] = np.broadcast_to(np.asarray(sgu_b[l], np.float32).reshape(1, 512), (128, 512))
    c[:, L * LC:L * LC + 8] = col(final_norm_g)
    inv = np.zeros((4, 16), np.float32)
    for g, win in enumerate(WINS):
        inv[g] = float(win) / np.minimum(np.arange(16) + 1, win)
    c[:, L * LC + 8:L * LC + 72] = np.broadcast_to(inv.reshape(1, 64), (128, 64))
    c[:, L * LC + 72] = EPS
    return c


_cache = {}


def run(inputs, S, L, n_cores):
    f = lambda a: np.ascontiguousarray(np.asarray(a, dtype=np.float32))
    key = (S, L)
    if key not in _cache:
        _cache[key] = build(S, L)
    nc = _cache[key]
    consts = make_consts(L, *[np.asarray(inputs[k], np.float32) for k in
                              ("norm_mix_g", "norm_xattn_g", "norm_ffn_g", "mem_norm_g", "pool_scale", "sgu_g", "conv_w", "conv_b", "sgu_b", "final_norm_g")])
    ident = np.eye(128, dtype=np.float32)
    tril = np.triu(np.ones((128, 128), np.float32))
    shared = {k: f(inputs[k]) for k in ("w_in", "pool_w", "sgu_w", "w_out", "wq", "wk", "wv", "wo", "w_up", "w_down")}
    shared.update(consts=consts, ident=ident, trilmask=tril)
    x = f(inputs["x"])
    mem = f(inputs["mem"])
    in_maps = []
    for c in range(n_cores):
        m = dict(shared)
        m["x"] = x[c]
        m["mem"] = mem[c]
        in_maps.append(m)
    res = run_bass_kernel_spmd(nc, in_maps, core_ids=list(range(n_cores)))
    return np.stack([np.asarray(r["out"], dtype=np.float32) for r in res.results], axis=0)


def kernel(**inputs):
    return run(inputs, 8192, 2, 8)
```

```python
import contextlib
import numpy as np
import concourse.bass as bass
import concourse.mybir as mybir
from concourse.bass_utils import run_bass_kernel_spmd

F32 = mybir.dt.float32
BF16 = mybir.dt.bfloat16
AF = mybir.ActivationFunctionType
ALU = mybir.AluOpType

D = 1024
KC = 8
NMEM = 256
DFF = 2816
NFC = 44
SUB = 512
NS = 2
T = SUB * NS
EPS = 1e-6
WINS = (2, 4, 8, 16)
NSLOT = 4
CONV_AHEAD = 6
NCS = 12
PIECE = 4096

LC = 728
O_GMIX, O_GX, O_GFFN, O_GMEM, O_PSC, O_SGG, O_CW0, O_CW1, O_CW2, O_CB, O_BBC = 0, 8, 16, 24, 32, 36, 40, 84, 128, 172, 216


class Eng:
    def __init__(self, name):
        self.name = name
        self.sem = None
        self.cnt = 0
        self.known = {}
        self.ops = []


class Buf:
    registry = {}

    def __init__(self, name, arena=None, lo=0, hi=0):
        self.name = name
        self.w = None
        self.r = {}
        self.arena = arena
        self.lo, self.hi = lo, hi
        self.al = []
        if arena is not None:
            lst = Buf.registry.setdefault(arena, [])
            for o in lst:
                if o.lo < hi and lo < o.hi:
                    o.al.append(self)
                    self.al.append(o)
            lst.append(self)


def _need(E, waits, ev, same_ok):
    if ev is None:
        return
    sem, val, owner = ev
    if same_ok and owner is E and E.name == "pe":
        return
    k = id(sem)
    if E.known.get(k, 0) >= val:
        return
    if k not in waits or waits[k][1] < val:
        waits[k] = (sem, val)


def _collect(E, reads, writes):
    waits = {}
    for b in reads:
        _need(E, waits, b.w, False)
    for b in writes:
        for x in [b] + b.al:
            _need(E, waits, x.w, True)
            for ev in x.r.values():
                _need(E, waits, ev, True)
    for k, (sem, val) in waits.items():
        E.known[k] = val
        E.ops.append(lambda e, s=sem, v=val: e.wait_ge(s, v))


STAGE = ["init"]
PE_LOG = []


def emit(E, fn, reads=(), writes=(), nmm=0):
    if nmm:
        PE_LOG.append((STAGE[0], nmm))
    _collect(E, reads, writes)
    E.cnt += 1
    ev = (E.sem, E.cnt, E)
    sem = E.sem
    E.ops.append(lambda e: fn(e).then_inc(sem, 1))
    for b in reads:
        b.r[id(sem)] = ev
    for b in writes:
        b.w = ev
        b.r = {}
    return ev


class DmaSem:
    def __init__(self, sem):
        self.sem = sem
        self.val = 0


def emit_dma(Q, fns, dsem, reads=(), writes=()):
    _collect(Q, reads, writes)
    dsem.val += 16 * len(fns)
    ev = (dsem.sem, dsem.val, None)
    s = dsem.sem
    for fn in fns:
        Q.ops.append(lambda e, fn=fn: fn(e).then_inc(s, 16))
    for b in reads:
        b.r[id(s)] = ev
    for b in writes:
        b.w = ev
        b.r = {}
    return ev


def build(S, L):
    NT = S // T
    del PE_LOG[:]
    STAGE[0] = "init"
    Buf.registry = {}
    nc = bass.Bass("TRN2", target_bir_lowering=False)
    dt_in = lambda name, shape: nc.dram_tensor(name, list(shape), F32, kind="ExternalInput").ap()
    x_d = dt_in("x", (S, D))
    mem_d = dt_in("mem", (NMEM, D))
    w_in_d = dt_in("w_in", (L, D, 1536))
    pool_w_d = dt_in("pool_w", (L, 4, 128, 128))
    sgu_w_d = dt_in("sgu_w", (L, 4, 128, 128))
    w_out_d = dt_in("w_out", (L, D, D))
    wq_d = dt_in("wq", (L, D, D))
    wk_d = dt_in("wk", (L, D, D))
    wv_d = dt_in("wv", (L, D, D))
    wo_d = dt_in("wo", (L, D, D))
    w_up_d = dt_in("w_up", (L, D, 2 * DFF))
    w_down_d = dt_in("w_down", (L, DFF, D))
    NC_ = L * LC + 8 + 64 + 1
    O_GFIN = L * LC
    O_INV = L * LC + 8
    O_EPS = L * LC + 72
    consts_d = dt_in("consts", (128, NC_))
    ident_d = dt_in("ident", (128, 128))
    mask_d = dt_in("trilmask", (128, 128))
    out_d = nc.dram_tensor("out", [S, D], F32, kind="ExternalOutput").ap()

    pieces = []

    def colpiece(W, c0):
        return [(lambda dst: dst.rearrange("p (k c) -> p k c", k=8),
                 W[:, c0:c0 + 512].rearrange("(k p) c -> p k c", p=128))]

    pidx = {}
    for l in range(L):
        for nm, W in (("wk", wk_d), ("wv", wv_d)):
            for pc in range(2):
                pidx[(nm, l, pc)] = len(pieces)
                pieces.append(colpiece(W[l], pc * 512))
    for l in range(L):
        for pc in range(3):
            pidx[("w_in", l, pc)] = len(pieces)
            pieces.append(colpiece(w_in_d[l], pc * 512))
        for nm, W in (("w_out", w_out_d), ("wq", wq_d), ("wo", wo_d)):
            for pc in range(2):
                pidx[(nm, l, pc)] = len(pieces)
                pieces.append(colpiece(W[l], pc * 512))
        for i in range(11):
            pidx[("w_up", l, i)] = len(pieces)
            pieces.append([
                (lambda dst: dst.rearrange("p (k c) -> p k c", k=8)[:, :, 0:256],
                 w_up_d[l][:, 256 * i:256 * i + 256].rearrange("(k p) c -> p k c", p=128)),
                (lambda dst: dst.rearrange("p (k c) -> p k c", k=8)[:, :, 256:512],
                 w_up_d[l][:, DFF + 256 * i:DFF + 256 * i + 256].rearrange("(k p) c -> p k c", p=128)),
            ])
        for m in range(8):
            pidx[("w_down", l, m)] = len(pieces)
            pieces.append([(lambda dst: dst[:, 0:2816].rearrange("p (k c) -> p k c", k=22),
                            w_down_d[l][:, m * 128:(m + 1) * 128].rearrange("(k p) c -> p k c", p=128))])
    NP = len(pieces)
    plen = {pi: PIECE for pi in range(NP)}
    for l in range(L):
        for m in range(8):
            plen[pidx[("w_down", l, m)]] = 22 * 128
    wscr = nc.dram_tensor("wscr", [NP, 128, PIECE], BF16, kind="Internal").ap()

    PE, ACT, DVE, POOL, SP = Eng("pe"), Eng("act"), Eng("dve"), Eng("pool"), Eng("sp")
    engs = [PE, ACT, DVE, POOL, SP]

    with contextlib.ExitStack() as es:
        def sb(name, shape, dt):
            return es.enter_context(nc.sbuf_tensor(name, list(shape), dt))

        for e in engs:
            e.sem = es.enter_context(nc.semaphore("s_" + e.name))

        def dsem(name):
            return DmaSem(es.enter_context(nc.semaphore(name)))

        h4 = sb("h4", (128, NS, KC, SUB), F32)
        xn = sb("xn", (128, NS, KC, SUB), BF16)
        arena = [sb("arena%d" % s, (128, 16896), BF16) for s in range(NS)]
        shr = sb("shr", (128, 7296), BF16)
        sqr = sb("sqr", (128, 4, SUB), BF16)
        rstd = sb("rstd", (128, NS, SUB), F32)
        stg = sb("stg", (128, 2, D), F32)
        KT = sb("KT", (128, L, KC, NMEM), BF16)
        Vt = sb("Vt", (128, L, 2, D), BF16)
        cst = sb("cst", (128, NC_), F32)
        poolw = sb("poolw", (128, L * 4, 128), BF16)
        wmT = sb("wmT", (128, L * 4, 128), BF16)
        ident = sb("ident_sb", (128, 128), F32)
        mask = sb("mask_sb", (128, 128), F32)
        onesD = sb("onesD", (128, 128), BF16)
        ones1 = sb("ones1", (128, 128), BF16)
        hs = sb("hs", (128, 2, L, 22, 2, 2), F32)
        phalo = sb("phalo", (128, L, 4, 16), F32)
        small = sb("small", (128, 256), F32)
        ring = sb("ring", (128, NSLOT, PIECE), BF16)
        banks = [es.enter_context(nc.psum_tensor("bank%d" % i, [128, SUB], F32)) for i in range(8)]
        bankb = [Buf("bank%d" % i) for i in range(8)]
        bank_rr = [0]

        def next_bank():
            i = bank_rr[0]
            bank_rr[0] = (i + 1) % 8
            return banks[i], bankb[i]

        hb = [[Buf("h%d_%d" % (s, k)) for k in range(KC)] for s in range(NS)]
        xnb = [[Buf("xn%d_%d" % (s, k)) for k in range(KC)] for s in range(NS)]
        sqb = [Buf("sq%d" % i) for i in range(4)]
        rstdb = [Buf("rstd%d" % s) for s in range(NS)]
        stgb = [Buf("stg%d" % i) for i in range(2)]
        stghb = [[Buf("stg%d_h%d" % (i, hf)) for hf in range(2)] for i in range(2)]
        stg_st = [dsem("stg_st%d" % i) for i in range(4)]
        KTb = [Buf("KT%d" % l) for l in range(L)]
        Vb = [Buf("V%d" % l) for l in range(L)]
        cstb, poolwb, wmTb, identb, maskb, onesb = Buf("cst"), Buf("poolw"), Buf("wmT"), Buf("ident"), Buf("mask"), Buf("ones")
        hsb = [[[Buf("hs%d_%d_%d" % (q, l, c)) for c in range(NFC)] for l in range(L)] for q in range(2)]
        phb = [Buf("phalo%d" % l) for l in range(L)]
        ringb = [Buf("ring%d" % i) for i in range(NSLOT)]
        ring_sem = [dsem("ring%d" % i) for i in range(NSLOT)]
        csem = dsem("csem")
        sgwsem = dsem("sgwsem")
        memsem = dsem("memsem")
        pwsem = dsem("pwsem")
        conv_sem = [dsem("conv%d" % i) for i in range(NCS)]
        convb = [Buf("convsem%d" % i) for i in range(NCS)]
        scrb = [Buf("scr%d" % i) for i in range(NP)]

        def aview(s, lo, hi, dt=BF16):
            v = arena[s][:, lo // 2:hi // 2]
            return v.bitcast(F32) if dt == F32 else v

        def abuf(name, s, lo, hi):
            return Buf("%s_%d" % (name, s), arena=("arena", s), lo=lo, hi=hi)

        actv = [aview(s, 0, 22528).rearrange("p (k c) -> p k c", k=22) for s in range(NS)]
        actb = [[abuf("act%d" % k, s, k * 1024, (k + 1) * 1024) for k in range(22)] for s in range(NS)]
        cav = [[aview(s, 22528 + i * 2048, 22528 + (i + 1) * 2048, F32) for i in range(4)] for s in range(NS)]
        cab = [[abuf("ca%d" % i, s, 22528 + i * 2048, 22528 + (i + 1) * 2048) for i in range(4)] for s in range(NS)]
        capv = [[aview(s, 22528 + r * 4096, 22528 + (r + 1) * 4096, F32).rearrange("p (g c) -> p g c", g=2) for r in range(2)] for s in range(NS)]
        sgv = [aview(s, 30720, 32768, F32) for s in range(NS)]
        sgb = [abuf("sg", s, 30720, 32768) for s in range(NS)]
        a8v = [aview(s, 0, 8192).rearrange("p (k c) -> p k c", k=8) for s in range(NS)]
        a8b = [[abuf("a8_%d" % k, s, k * 1024, (k + 1) * 1024) for k in range(8)] for s in range(NS)]
        pv = [aview(s, 8192, 8192 + 8448, F32).rearrange("p (g c) -> p g c", g=4) for s in range(NS)]
        pb = [[abuf("p%d" % g, s, 8192 + g * 2112, 8192 + (g + 1) * 2112) for g in range(4)] for s in range(NS)]
        uv = [aview(s, 16640, 20736).rearrange("p (k c) -> p k c", k=4) for s in range(NS)]
        ub = [[abuf("u%d" % k, s, 16640 + k * 1024, 16640 + (k + 1) * 1024) for k in range(4)] for s in range(NS)]
        ddv = [aview(s, 20736, 24832).rearrange("p (k c) -> p k c", k=4) for s in range(NS)]
        ddb = [[abuf("d%d" % k, s, 20736 + k * 1024, 20736 + (k + 1) * 1024) for k in range(4)] for s in range(NS)]
        vnv = [aview(s, 24832, 28928).rearrange("p (k c) -> p k c", k=4) for s in range(NS)]
        vnb = [[abuf("vn%d" % k, s, 24832 + k * 1024, 24832 + (k + 1) * 1024) for k in range(4)] for s in range(NS)]
        xbv = [aview(s, 28928, 33024).rearrange("p (k c) -> p k c", k=4) for s in range(NS)]
        xbb = [[abuf("xb%d" % k, s, 28928 + k * 1024, 28928 + (k + 1) * 1024) for k in range(4)] for s in range(NS)]

        def sview(lo, hi, dt=BF16):
            v = shr[:, lo // 2:hi // 2]
            return v.bitcast(F32) if dt == F32 else v

        def sbuf_(name, lo, hi):
            return Buf(name, arena=("shr", 0), lo=lo, hi=hi)

        ptv = [sview(i * 2112, (i + 1) * 2112, F32) for i in range(2)]
        ptb = [sbuf_("pt%d" % i, i * 2112, (i + 1) * 2112) for i in range(2)]
        vfv = [sview(4224 + i * 2048, 4224 + (i + 1) * 2048, F32) for i in range(3)]
        vfb = [sbuf_("vf%d" % i, 4224 + i * 2048, 4224 + (i + 1) * 2048) for i in range(3)]
        ptv2 = [sview(10368 + i * 2112, 10368 + (i + 1) * 2112, F32) for i in range(2)]
        ptb2 = [sbuf_("pt2_%d" % i, 10368 + i * 2112, 10368 + (i + 1) * 2112) for i in range(2)]
        sgtv = [sview(10368 + i * 2048, 12416 + i * 2048, F32) for i in range(2)]
        sgtb = [sbuf_("sgt%d" % i, 10368 + i * 2048, 12416 + i * 2048) for i in range(2)]
        PTv = [sview(i * 1024, (i + 1) * 1024) for i in range(6)]
        PTb = [sbuf_("PT%d" % i, i * 1024, (i + 1) * 1024) for i in range(6)]
        rdv = [sview(6144 + i * 2048, 6144 + (i + 1) * 2048, F32) for i in range(2)]
        rdb = [sbuf_("rd%d" % i, 6144 + i * 2048, 6144 + (i + 1) * 2048) for i in range(2)]
        NXS = 3
        xsv = sview(0, NXS * 4096, F32).rearrange("p (i f) -> p i f", i=NXS)
        xsb = [sbuf_("xs%d" % i, i * 4096, (i + 1) * 4096) for i in range(NXS)]
        xs_ld = [dsem("xs_ld%d" % i) for i in range(NXS)]
        smallb = [Buf("small%d" % i) for i in range(16)]

        C = lambda off, n=1: cst[:, off:off + n]

        seq = []
        for ti in range(NT):
            for l in range(L):
                seq += [pidx[("w_in", l, 0)], pidx[("w_in", l, 2)], pidx[("w_in", l, 1)]]
                for nm in ("w_out", "wq") + (("wk", "wv") if ti == 0 else ()) + ("wo",):
                    seq += [pidx[(nm, l, 0)], pidx[(nm, l, 1)]]
                seq += [pidx[("w_up", l, i)] for i in range(11)]
                seq += [pidx[("w_down", l, m)] for m in range(8)]
        ws = {"load": 0, "use": 0}

        def ws_load_next():
            n = ws["load"]
            if n >= len(seq):
                return
            ws["load"] = n + 1
            s = n % NSLOT
            pi = seq[n]
            conv_issue_upto(n + 1 + CONV_AHEAD)
            n_el = plen[pi]
            emit_dma(SP, [lambda e, s=s, pi=pi, n_el=n_el: e.dma_start(out=ring[:, s, 0:n_el], in_=wscr[pi][:, 0:n_el])],
                     ring_sem[s], reads=[scrb[pi]], writes=[ringb[s]])

        def ws_acquire(expect, ahead=0):
            n = ws["use"] + ahead
            assert seq[n] == expect, (n, seq[n], expect)
            while ws["load"] <= n:
                ws_load_next()
            s = n % NSLOT
            return ring[:, s, :], ringb[s]

        def ws_release():
            ws["use"] += 1
            while ws["load"] < min(len(seq), ws["use"] + NSLOT):
                ws_load_next()

        def mm_group(pairs, reads, bank=None):
            if bank is None:
                bank = next_bank()
            bt, bb = bank
            n = len(pairs)

            def fn(e):
                for i, (l_, r_) in enumerate(pairs):
                    ins = e.matmul(bt[:, 0:r_.shape[-1]] if r_.shape[-1] != SUB else bt[:], l_, r_, start=(i == 0), stop=(i == n - 1))
                return ins
            emit(PE, fn, reads=reads, writes=[bb], nmm=n)
            return bt, bb

        small_rr = {"ln": [0, 0, 4], "fix": [0, 4, 8], "misc": [0, 12, 4]}

        def next_small(kind="misc"):
            st = small_rr[kind]
            i = st[1] + st[0]
            st[0] = (st[0] + 1) % st[2]
            return small[:, i * 16:(i + 1) * 16], smallb[i]

        emit_dma(SP, [lambda e: e.dma_start(out=cst[:], in_=consts_d),
                      lambda e: e.dma_start(out=ident[:], in_=ident_d),
                      lambda e: e.dma_start(out=mask[:], in_=mask_d)], csem, writes=[cstb, identb, maskb])
        emit(POOL, lambda e: e.memset(onesD[:], 1.0 / D), writes=[onesb])
        emit(POOL, lambda e: e.memset(ones1[:], 1.0), writes=[onesb])
        emit(POOL, lambda e: e.memset(hs[:].rearrange("p a l j g t -> p (a l j g t)"), 0.0),
             writes=[b for q in range(2) for l in range(L) for b in hsb[q][l]])
        emit(POOL, lambda e: e.memset(phalo[:].rearrange("p l g c -> p (l g c)"), 0.0), writes=phb)
        emit_dma(POOL, [lambda e: e.dma_start(out=poolw[:], in_=pool_w_d.rearrange("l g c d -> c (l g) d"))],
                 pwsem, writes=[poolwb])
        order = []
        for pi in seq:
            if pi not in order:
                order.append(pi)
        conv_next = [0]

        def conv_issue_upto(k):
            while conv_next[0] < min(k, len(order)):
                n = conv_next[0]
                pi = order[n]
                fns = [(lambda e, dfn=dfn, src=src, pi=pi: e.dma_start(out=dfn(wscr[pi]), in_=src)) for dfn, src in pieces[pi]]
                emit_dma(POOL, fns, conv_sem[n % NCS], writes=[scrb[pi], convb[n % NCS]])
                conv_next[0] = n + 1

        sgw_v = aview(0, 8192, 8192 + L * 2048, F32).rearrange("p (a s) -> p a s", a=L * 4)
        sgw_b = abuf("sgw", 0, 8192, 8192 + L * 2048)
        emit_dma(SP, [lambda e: e.dma_start(out=sgw_v, in_=sgu_w_d.rearrange("l h t s -> t (l h) s"))], sgwsem, writes=[sgw_b])
        for a0 in range(0, L * 4, 4):
            bt, bb = next_bank()

            def fn(e, a0=a0, bt=bt):
                for j in range(4):
                    ins = e.transpose(bt[:, j * 128:(j + 1) * 128], sgw_v[:, a0 + j, :], ident[:])
                return ins
            emit(PE, fn, reads=[sgw_b, identb], writes=[bb], nmm=4)
            emit(DVE, lambda e, a0=a0, bt=bt: e.tensor_tensor(
                out=wmT[:, a0:a0 + 4, :], in0=bt[:].rearrange("p (a t) -> p a t", a=4),
                in1=mask[:, None, :].broadcast_to([128, 4, 128]), op=ALU.mult), reads=[bb, maskb], writes=[wmTb])

        memv = stg
        emit_dma(SP, [lambda e: e.dma_start(out=stg[:], in_=mem_d.rearrange("(c p) f -> p c f", p=128))], memsem, writes=[stgb[0], stgb[1]])
        for c in range(2):
            sv, sbf = next_small()
            emit(DVE, lambda e, sv=sv, c=c: e.bn_stats(out=sv[:, 0:6], in_=memv[:, c, 0:512]), reads=[stgb[c]], writes=[sbf])
            emit(DVE, lambda e, sv=sv, c=c: e.bn_stats(out=sv[:, 6:12], in_=memv[:, c, 512:1024]), reads=[stgb[c]], writes=[sbf])
            emit(DVE, lambda e, sv=sv: e.bn_aggr(out=sv[:, 12:14], in_=sv[:, 0:12]), reads=[sbf], writes=[sbf])
            emit(DVE, lambda e, sv=sv: e.scalar_tensor_tensor(out=sv[:, 14:15], in0=sv[:, 12:13], scalar=sv[:, 12:13], in1=sv[:, 13:14],
                                                             op0=ALU.mult, op1=ALU.add), reads=[sbf], writes=[sbf])
            emit(ACT, lambda e, sv=sv: e.activation(out=sv[:, 15:16], in_=sv[:, 14:15], func=AF.Sqrt, bias=EPS), reads=[sbf], writes=[sbf])
            emit(DVE, lambda e, sv=sv: e.reciprocal(out=sv[:, 15:16], in_=sv[:, 15:16]), reads=[sbf], writes=[sbf])
            emit(DVE, lambda e, sv=sv, c=c: e.tensor_scalar(out=memv[:, c, :], in0=memv[:, c, :], scalar1=sv[:, 15:16], scalar2=None, op0=ALU.mult),
                 reads=[sbf, stgb[c]], writes=[stgb[c]])
        def kv_compute(l):
            mT_v = aview(0, 8192, 12288).rearrange("p (k m) -> p k m", k=8)
            mT_b = abuf("memnT%d" % l, 0, 8192, 12288)
            for c in range(2):
                for k0 in (0, 4):
                    bt, bb = next_bank()

                    def fn(e, c=c, k0=k0, bt=bt):
                        for j in range(4):
                            ins = e.transpose(bt[:, j * 128:(j + 1) * 128], memv[:, c, (k0 + j) * 128:(k0 + j + 1) * 128], ident[:])
                        return ins
                    emit(PE, fn, reads=[stgb[c], identb], writes=[bb], nmm=4)
                    emit(DVE, lambda e, c=c, k0=k0, bt=bt, l=l, mT_v=mT_v: e.tensor_tensor(
                        out=mT_v[:, k0:k0 + 4, c * 128:(c + 1) * 128], in0=bt[:].rearrange("p (k m) -> p k m", k=4),
                        in1=cst[:, l * LC + O_GMEM + k0:l * LC + O_GMEM + k0 + 4].unsqueeze(2).broadcast_to([128, 4, 128]), op=ALU.mult),
                        reads=[bb, cstb], writes=[mT_b])
            for pc in range(2):
                slot, slb = ws_acquire(pidx[("wk", l, pc)])
                s3 = slot.rearrange("p (k c) -> p k c", k=8)
                for mm in range(4):
                    dch = pc * 4 + mm
                    bt, bb = mm_group([(s3[:, k, mm * 128:(mm + 1) * 128], mT_v[:, k, :]) for k in range(8)], reads=[slb, mT_b])
                    emit(ACT, lambda e, bt=bt, l=l, dch=dch: e.activation(out=KT[:, l, dch, :], in_=bt[:, 0:NMEM], func=AF.Copy),
                         reads=[bb], writes=[KTb[l]])
                ws_release()
            for pc in range(2):
                slot, slb = ws_acquire(pidx[("wv", l, pc)])
                s3 = slot.rearrange("p (k c) -> p k c", k=8)
                for mc in range(2):
                    bt, bb = mm_group([(mT_v[:, k, mc * 128:(mc + 1) * 128], s3[:, k, :]) for k in range(8)], reads=[slb, mT_b])
                    emit(ACT, lambda e, bt=bt, l=l, mc=mc, pc=pc: e.activation(out=Vt[:, l, mc, pc * 512:(pc + 1) * 512], in_=bt[:], func=AF.Copy),
                         reads=[bb], writes=[Vb[l]])
                ws_release()

        sq_rr = [0]

        def norm_stats(s):
            bank = next_bank()
            bt, bb = bank
            for k in range(KC):
                i = sq_rr[0]
                sq_rr[0] = (i + 1) % 4
                emit(ACT, lambda e, i=i, k=k: e.activation(out=sqr[:, i, :], in_=h4[:, s, k, :], func=AF.Square),
                     reads=[hb[s][k]], writes=[sqb[i]])
                emit(PE, lambda e, i=i, k=k: e.matmul(bt[:], onesD[:], sqr[:, i, :], start=(k == 0), stop=(k == KC - 1)),
                     reads=[sqb[i], onesb], writes=[bb], nmm=1)
            emit(ACT, lambda e: e.activation(out=rstd[:, s, :], in_=bt[:], func=AF.Ln, bias=C(O_EPS)), reads=[bb, cstb], writes=[rstdb[s]])
            emit(ACT, lambda e: e.activation(out=rstd[:, s, :], in_=rstd[:, s, :], func=AF.Exp, scale=-0.5), reads=[rstdb[s]], writes=[rstdb[s]])

        xn_pend = {}

        def xn_emit(s, n):
            st = xn_pend.get(s)
            while st is not None and n > 0 and st[1] < KC:
                goff, k = st
                emit(DVE, lambda e, k=k, goff=goff: e.scalar_tensor_tensor(out=xn[:, s, k, :], in0=h4[:, s, k, :], scalar=C(goff + k),
                                                                           in1=rstd[:, s, :], op0=ALU.mult, op1=ALU.mult),
                     reads=[hb[s][k], rstdb[s], cstb], writes=[xnb[s][k]])
                st[1] += 1
                n -= 1
            if st is not None and st[1] >= KC:
                del xn_pend[s]

        def xn_flush(s):
            xn_emit(s, KC)

        def norm_begin(s, goff):
            norm_stats(s)
            xn_pend[s] = [goff, 0]

        def load_x_dma(ti, tcg):
            i = tcg % NXS
            r0 = ti * T + tcg * 128
            emit_dma(SP, [lambda e, i=i, r0=r0: e.dma_start(out=xsv[:, i, :], in_=x_d[r0:r0 + 128, :])], xs_ld[i], writes=[xsb[i]])

        def load_x_chunk(ti, tcg, after):
            s, tc = divmod(tcg, SUB // 128)
            i = tcg % NXS
            b0, b1 = next_bank(), next_bank()

            def fn(e, i=i, b0=b0, b1=b1):
                for kc in range(KC):
                    bt = (b0 if kc < 4 else b1)[0]
                    ins = e.transpose(bt[:, (kc % 4) * 128:(kc % 4 + 1) * 128], xsv[:, i, kc * 128:(kc + 1) * 128], ident[:])
                return ins
            emit(PE, fn, reads=[xsb[i], identb], writes=[b0[1], b1[1]], nmm=8)
            if tcg + NXS < T // 128:
                load_x_dma(ti, tcg + NXS)
            emit(ACT, lambda e: e.activation(out=h4[:, s, 0:4, tc * 128:(tc + 1) * 128],
                                             in_=b0[0][:].rearrange("p (k t) -> p k t", k=4), func=AF.Copy),
                 reads=[b0[1]], writes=hb[s][0:4])
            emit(DVE, lambda e: e.tensor_copy(out=h4[:, s, 4:8, tc * 128:(tc + 1) * 128],
                                              in_=b1[0][:].rearrange("p (k t) -> p k t", k=4)),
                 reads=[b1[1]], writes=hb[s][4:8])
            if s == 1:
                xn_emit(0, 2)
            if tc == SUB // 128 - 1:
                after(s)

        vf_rr = [0]

        def mixer_p(s, l, slot, slb, first, between=None):
            s3 = slot.rearrange("p (k c) -> p k c", k=8)
            emit(POOL, lambda e: e.tensor_copy(out=pv[s][:, :, 0:16], in_=phalo[:, l, :, :]), reads=[phb[l]], writes=pb[s])
            for m in range(4):
                bt, bb = mm_group([(s3[:, k, m * 128:(m + 1) * 128], xn[:, s, k, :]) for k in range(KC)], reads=[slb] + xnb[s])
                emit(ACT, lambda e, bt=bt, m=m: e.activation(out=pv[s][:, m, 16:528], in_=bt[:], func=AF.Identity, scale=1.0 / WINS[m], bias=0.0),
                     reads=[bb], writes=[pb[s][m]])
                emit(ACT, lambda e, bt=bt, m=m: e.activation(out=xbv[s][:, m, :], in_=bt[:], func=AF.Copy), reads=[bb], writes=[xbb[s][m]])
                if between is not None:
                    between()
            emit(POOL, lambda e: e.tensor_copy(out=phalo[:, l, :, :], in_=pv[s][:, :, 512:528]), reads=pb[s], writes=[phb[l]])

        pool_q = []

        def pool_drain(n=10 ** 6):
            while pool_q and n > 0:
                pool_q.pop(0)()
                n -= 1

        def pooling(s, l, first, POOL=POOL, tv=None, tb=None, queue=False):
            if queue:
                real_emit = emit

                def q_emit(E, fn, reads=(), writes=()):
                    pool_q.append(lambda: real_emit(E, fn, reads=reads, writes=writes))
                return _pooling(s, l, first, POOL, tv, tb, q_emit)
            return _pooling(s, l, first, POOL, tv, tb, emit)

        def _pooling(s, l, first, POOL, tv, tb, emit):
            tv = ptv if tv is None else tv
            tb = ptb if tb is None else tb
            for g, win in enumerate(WINS):
                X = pv[s][:, g, :]
                t1, t2 = tv[0], tv[1]
                steps = [(1, t1, tb[0]), (2, t2, tb[1]), (4, t1, tb[0]), (8, t2, tb[1])][:g + 1]
                src, srcb, lo = X, pb[s][g], 0
                for (sh, dst, dstb) in steps:
                    nlo = lo + sh
                    emit(POOL, lambda e, src=src, dst=dst, nlo=nlo, sh=sh: e.tensor_tensor(
                        out=dst[:, nlo:528], in0=src[:, nlo:528], in1=src[:, nlo - sh:528 - sh], op=ALU.add),
                        reads=[srcb], writes=[dstb])
                    src, srcb, lo = dst, dstb, nlo
                emit(POOL, lambda e, src=src, g=g: e.tensor_tensor(out=ddv[s][:, g, :], in0=src[:, 16:528], in1=xbv[s][:, g, :], op=ALU.subtract),
                     reads=[srcb, xbb[s][g]], writes=[ddb[s][g]])
                if first:
                    sv, sbf = next_small()
                    emit(POOL, lambda e, src=src, sv=sv, g=g: e.tensor_tensor(out=sv[:, 0:16], in0=src[:, 16:32], in1=C(O_INV + g * 16, 16), op=ALU.mult),
                         reads=[srcb, cstb], writes=[sbf])
                    emit(POOL, lambda e, sv=sv, g=g: e.tensor_tensor(out=ddv[s][:, g, 0:16], in0=sv[:, 0:16], in1=xbv[s][:, g, 0:16], op=ALU.subtract),
                         reads=[sbf, xbb[s][g]], writes=[ddb[s][g]])

        def mixer_v(s, l, slot, slb):
            s3 = slot.rearrange("p (k c) -> p k c", k=8)
            pend = []

            def ln_rstd(p_):
                sv, sbf, i, tc = p_
                emit(ACT, lambda e: e.activation(out=sv[:, 8:9], in_=sv[:, 7:8], func=AF.Sqrt, bias=C(O_EPS)), reads=[sbf, cstb], writes=[sbf])

            def normalize(p_):
                sv, sbf, i, tc = p_
                emit(DVE, lambda e: e.reciprocal(out=sv[:, 9:10], in_=sv[:, 8:9]), reads=[sbf], writes=[sbf])
                emit(DVE, lambda e: e.tensor_scalar(out=vnv[s][:, tc, :], in0=vfv[i], scalar1=sv[:, 6:7], scalar2=sv[:, 9:10],
                                                    op0=ALU.subtract, op1=ALU.mult), reads=[sbf, vfb[i]], writes=[vnb[s][tc]])
            for tc in range(4):
                bt, bb = mm_group([(xn[:, s, k, tc * 128:(tc + 1) * 128], s3[:, k, :]) for k in range(KC)], reads=[slb] + xnb[s])
                i = vf_rr[0]
                vf_rr[0] = (i + 1) % 3
                emit(ACT, lambda e, bt=bt, i=i: e.activation(out=vfv[i], in_=bt[:], func=AF.Gelu), reads=[bb], writes=[vfb[i]])
                sv, sbf = next_small("ln")
                emit(DVE, lambda e, sv=sv, i=i: e.bn_stats(out=sv[:, 0:6], in_=vfv[i]), reads=[vfb[i]], writes=[sbf])
                emit(DVE, lambda e, sv=sv: e.bn_aggr(out=sv[:, 6:8], in_=sv[:, 0:6]), reads=[sbf], writes=[sbf])
                pend.append((sv, sbf, i, tc))
                if tc == 2:
                    ln_rstd(pend[0])
                    ln_rstd(pend[1])
                    normalize(pend.pop(0))
                    normalize(pend.pop(0))
                pool_drain(1)
            for p_ in pend:
                ln_rstd(p_)
            while pend:
                normalize(pend.pop(0))

        def mixer_u(s, l, slot, slb):
            s3 = slot.rearrange("p (k c) -> p k c", k=8)
            for m in range(4):
                bt, bb = mm_group([(s3[:, k, m * 128:(m + 1) * 128], xn[:, s, k, :]) for k in range(KC)], reads=[slb] + xnb[s])
                emit(ACT, lambda e, bt=bt, m=m: e.activation(out=uv[s][:, m, :], in_=bt[:], func=AF.Gelu), reads=[bb], writes=[ub[s][m]])
                pool_drain(1)

        def mixer_mix(s, l):
            base = l * LC
            for g in range(4):
                bt, bb = mm_group([(poolw[:, l * 4 + g, :], ddv[s][:, g, :])], reads=[poolwb, ddb[s][g]])
                emit(ACT, lambda e, bt=bt, g=g: e.activation(out=a8v[s][:, g, :], in_=bt[:], func=AF.Identity, scale=C(base + O_PSC + g), bias=0.0),
                     reads=[bb, cstb], writes=[a8b[s][g]])
            for hh in range(4):
                bt, bb = next_bank()

                def fn(e, bt=bt, hh=hh):
                    for tc in range(4):
                        ins = e.matmul(bt[:, tc * 128:(tc + 1) * 128], vnv[s][:, tc, hh * 128:(hh + 1) * 128], wmT[:, l * 4 + hh, :], start=True, stop=True)
                    return ins
                emit(PE, fn, reads=vnb[s] + [wmTb], writes=[bb], nmm=4)
                gi = hh % 2
                emit(DVE, lambda e, bt=bt, hh=hh, gi=gi: e.scalar_tensor_tensor(
                    out=sgtv[gi].rearrange("p (r t) -> p r t", r=4), in0=bt[:].rearrange("p (r t) -> p r t", r=4), scalar=C(base + O_SGG + hh),
                    in1=cst[:, base + O_BBC + hh * 128:base + O_BBC + (hh + 1) * 128][:, None, :].broadcast_to([128, 4, 128]),
                    op0=ALU.mult, op1=ALU.add), reads=[bb, cstb], writes=[sgtb[gi]])
                emit(POOL, lambda e, hh=hh, gi=gi: e.tensor_tensor(out=a8v[s][:, 4 + hh, :], in0=sgtv[gi], in1=uv[s][:, hh, :], op=ALU.mult),
                     reads=[sgtb[gi], ub[s][hh]], writes=[a8b[s][4 + hh]])

        def proj_resid(s, pc, slot, slb, between=None):
            s3 = slot.rearrange("p (k c) -> p k c", k=8)
            for mm in range(4):
                m = pc * 4 + mm
                bt, bb = mm_group([(s3[:, k, mm * 128:(mm + 1) * 128], a8v[s][:, k, :]) for k in range(KC)], reads=[slb] + a8b[s])
                emit(DVE, lambda e, bt=bt, m=m: e.tensor_tensor(out=h4[:, s, m, :], in0=bt[:], in1=h4[:, s, m, :], op=ALU.add),
                     reads=[bb, hb[s][m]], writes=[hb[s][m]])
                if between is not None:
                    between()

        def proj_q(s, pc, slot, slb, between=None):
            s3 = slot.rearrange("p (k c) -> p k c", k=8)
            for mm in range(4):
                m = pc * 4 + mm
                bt, bb = mm_group([(s3[:, k, mm * 128:(mm + 1) * 128], xn[:, s, k, :]) for k in range(KC)], reads=[slb] + xnb[s])
                emit(ACT, lambda e, bt=bt, m=m: e.activation(out=a8v[s][:, m, :], in_=bt[:], func=AF.Copy), reads=[bb], writes=[a8b[s][m]])
                if between is not None:
                    between()

        def attention_all(l):
            units = [(s, hh) for hh in range(4) for s in range(NS)]

            def scores(u):
                s, hh = units[u]
                r = u % 3
                for mc in range(2):
                    bt, bb = mm_group([(KT[:, l, 2 * hh + dc, mc * 128:(mc + 1) * 128], a8v[s][:, 2 * hh + dc, :]) for dc in range(2)],
                                      reads=[KTb[l], a8b[s][2 * hh], a8b[s][2 * hh + 1]])
                    emit(ACT, lambda e, bt=bt, r=r, mc=mc: e.activation(out=PTv[2 * r + mc], in_=bt[:], func=AF.Exp, scale=1.0 / 16.0),
                         reads=[bb], writes=[PTb[2 * r + mc]])

            def rest(u):
                s, hh = units[u]
                r = u % 3
                rr = u % 2
                bt, bb = mm_group([(ones1[:], PTv[2 * r + mc]) for mc in range(2)], reads=[onesb, PTb[2 * r], PTb[2 * r + 1]])
                emit(ACT, lambda e, bt=bt, rr=rr: e.activation(out=rdv[rr], in_=bt[:], func=AF.Ln), reads=[bb], writes=[rdb[rr]])
                emit(ACT, lambda e, rr=rr: e.activation(out=rdv[rr], in_=rdv[rr], func=AF.Exp, scale=-1.0), reads=[rdb[rr]], writes=[rdb[rr]])
                for dc in range(2):
                    dch = 2 * hh + dc
                    bt, bb = mm_group([(Vt[:, l, mc, dch * 128:(dch + 1) * 128], PTv[2 * r + mc]) for mc in range(2)],
                                      reads=[Vb[l], PTb[2 * r], PTb[2 * r + 1]])
                    emit(DVE, lambda e, bt=bt, rr=rr, dch=dch, s=s: e.tensor_tensor(out=a8v[s][:, dch, :], in0=bt[:], in1=rdv[rr], op=ALU.mult),
                         reads=[bb, rdb[rr]], writes=[a8b[s][dch]])

            n = len(units)
            scores(0)
            scores(1)
            for u in range(n):
                if u + 2 < n:
                    scores(u + 2)
                rest(u)

        ca_rr = [0, 0]
        hs_par = {}

        ffn_pend = []

        def ffn_flush():
            while ffn_pend:
                ffn_pend.pop(0)()

        def ffn_up(s, l, i, slot, slb, between=None):
            base = l * LC
            s3 = slot.rearrange("p (k c) -> p k c", k=8)
            for j in range(2):
                jj = 2 * i + j
                r = ca_rr[s]
                ca_rr[s] ^= 1
                q = hs_par.get((l, jj), 0)
                hs_par[(l, jj)] = q ^ 1
                res = []
                for which, coff in ((0, j * 128), (1, 256 + j * 128)):
                    cf = jj + 22 * which
                    cc = 2 * jj + which
                    bt, bb = mm_group([(s3[:, k, coff:coff + 128], xn[:, s, k, :]) for k in range(KC)], reads=[slb] + xnb[s])
                    av, ab = cav[s][2 * r + which], cab[s][2 * r + which]
                    emit(ACT, lambda e, bt=bt, av=av, cc=cc: e.activation(out=av, in_=bt[:], func=AF.Identity, scale=C(base + O_CW2 + cc), bias=C(base + O_CB + cc)),
                         reads=[bb, cstb], writes=[ab])
                    emit(ACT, lambda e, bt=bt, jj=jj, which=which, q=q: e.activation(out=hs[:, q ^ 1, l, jj, which, :], in_=bt[:, 510:512], func=AF.Copy),
                         reads=[bb], writes=[hsb[q ^ 1][l][cf]])
                    emit(DVE, lambda e, bt=bt, av=av, cc=cc: e.scalar_tensor_tensor(out=av[:, 1:512], in0=bt[:, 0:511], scalar=C(base + O_CW1 + cc),
                                                                                 in1=av[:, 1:512], op0=ALU.mult, op1=ALU.add),
                         reads=[bb, ab, cstb], writes=[ab])
                    emit(DVE, lambda e, bt=bt, av=av, cc=cc: e.scalar_tensor_tensor(out=av[:, 2:512], in0=bt[:, 0:510], scalar=C(base + O_CW0 + cc),
                                                                                 in1=av[:, 2:512], op0=ALU.mult, op1=ALU.add),
                         reads=[bb, ab, cstb], writes=[ab])
                    res.append((av, ab))
                (ag, agb), (avv, avb) = res
                sv, sbf = next_small("fix")
                hsp = hs[:, q, l, jj, :, :]
                hsrd = [hsb[q][l][jj], hsb[q][l][22 + jj]]
                cw0p = cst[:, base + O_CW0 + 2 * jj:base + O_CW0 + 2 * jj + 2].unsqueeze(2)
                cw1p = cst[:, base + O_CW1 + 2 * jj:base + O_CW1 + 2 * jj + 2].unsqueeze(2)
                ap_ = capv[s][r]
                emit(POOL, lambda e, sv=sv, hsp=hsp, cw0p=cw0p: e.tensor_tensor(out=sv[:, 0:4].rearrange("p (g t) -> p g t", g=2), in0=hsp,
                                                                               in1=cw0p.broadcast_to([128, 2, 2]), op=ALU.mult),
                     reads=hsrd + [cstb], writes=[sbf])
                emit(POOL, lambda e, sv=sv, hsp=hsp, cw1p=cw1p: e.tensor_tensor(out=sv[:, 4:6].rearrange("p (g t) -> p g t", g=2), in0=hsp[:, :, 1:2],
                                                                               in1=cw1p, op=ALU.mult),
                     reads=hsrd + [cstb], writes=[sbf])
                emit(POOL, lambda e, sv=sv, ap_=ap_: e.tensor_tensor(out=ap_[:, :, 0:2], in0=ap_[:, :, 0:2], in1=sv[:, 0:4].rearrange("p (g t) -> p g t", g=2), op=ALU.add),
                     reads=[sbf, agb, avb], writes=[agb, avb])
                emit(POOL, lambda e, sv=sv, ap_=ap_: e.tensor_tensor(out=ap_[:, :, 0:1], in0=ap_[:, :, 0:1], in1=sv[:, 4:6].rearrange("p (g t) -> p g t", g=2), op=ALU.add),
                     reads=[sbf, agb, avb], writes=[agb, avb])

                def finish(ag=ag, agb=agb, avv=avv, avb=avb, jj=jj, s=s):
                    emit(ACT, lambda e: e.activation(out=sgv[s], in_=ag, func=AF.Silu), reads=[agb], writes=[sgb[s]])
                    emit(POOL, lambda e: e.tensor_tensor(out=actv[s][:, jj, :], in0=sgv[s], in1=avv, op=ALU.mult),
                         reads=[sgb[s], avb], writes=[actb[s][jj]])
                ffn_flush()
                ffn_pend.append(finish)
                if between is not None:
                    between()
                    between()

        def ffn_down(s, m, slot, slb, pre_evac=None, k0=0, k1=22):
            s3 = slot[:, 0:2816].rearrange("p (k c) -> p k c", k=22)
            bt, bb = mm_group([(s3[:, k, :], actv[s][:, k, :]) for k in range(k0, k1)], reads=[slb] + actb[s][k0:k1])
            if pre_evac is not None:
                pre_evac()
            emit(DVE, lambda e, bt=bt: e.tensor_tensor(out=h4[:, s, m, :], in0=bt[:], in1=h4[:, s, m, :], op=ALU.add),
                 reads=[bb, hb[s][m]], writes=[hb[s][m]])

        out_i = [0]

        def final_scale(s):
            for k in range(KC):
                emit(DVE, lambda e, k=k: e.scalar_tensor_tensor(out=h4[:, s, k, :], in0=h4[:, s, k, :], scalar=C(O_GFIN + k),
                                                              in1=rstd[:, s, :], op0=ALU.mult, op1=ALU.mult),
                     reads=[hb[s][k], rstdb[s], cstb], writes=[hb[s][k]])

        def final_store_chunk(s, ti, tc):
            i = out_i[0]
            out_i[0] ^= 1
            b0, b1 = next_bank(), next_bank()

            def fn(e, b0=b0, b1=b1, tc=tc):
                for kc in range(KC):
                    bt = (b0 if kc < 4 else b1)[0]
                    ins = e.transpose(bt[:, (kc % 4) * 128:(kc % 4 + 1) * 128], h4[:, s, kc, tc * 128:(tc + 1) * 128], ident[:])
                return ins
            emit(PE, fn, reads=hb[s] + [identb], writes=[b0[1], b1[1]], nmm=8)
            r0 = ti * T + s * SUB + tc * 128
            emit(ACT, lambda e: e.activation(out=stg[:, i, 0:512], in_=b0[0][:], func=AF.Copy), reads=[b0[1]], writes=[stgb[i], stghb[i][0]])
            emit_dma(ACT, [lambda e: e.dma_start(out=out_d[r0:r0 + 128, 0:512], in_=stg[:, i, 0:512])], stg_st[2 * i], reads=[stghb[i][0]])
            emit(DVE, lambda e: e.tensor_copy(out=stg[:, i, 512:1024], in_=b1[0][:]), reads=[b1[1]], writes=[stgb[i], stghb[i][1]])
            emit_dma(SP, [lambda e: e.dma_start(out=out_d[r0:r0 + 128, 512:1024], in_=stg[:, i, 512:1024])], stg_st[2 * i + 1], reads=[stghb[i][1]])

        def piece_step(key, fn, after=None):
            STAGE[0] = "%s%s" % (key[0], key[2] if key[0] in ("w_in",) else "")
            slot, slb = ws_acquire(pidx[key])
            for s in range(NS):
                fn(s, slot, slb)
                if after is not None:
                    after(s)
            ws_release()

        def xn_flush_0():
            xn_flush(0)

        def final_norm(s):
            norm_stats(s)
            final_scale(s)

        def norm_piece(key, fn, goff):
            STAGE[0] = key[0]
            slot, slb = ws_acquire(pidx[key])
            fn(0, slot, slb, None)
            norm_begin(0, goff)
            fn(1, slot, slb, lambda: xn_emit(0, 2))
            xn_flush(0)
            norm_begin(1, goff)
            xn_flush(1)
            ws_release()

        def first_piece(key, fn):
            STAGE[0] = "%s%s" % (key[0], key[2] if key[0] in ("w_in",) else "")
            slot, slb = ws_acquire(pidx[key])
            fn(0, slot, slb, lambda: xn_emit(1, 1))
            xn_flush(1)
            fn(1, slot, slb, None)
            ws_release()

        norm1_first = lambda s: norm_begin(s, O_GMIX)
        for ti in range(NT):
            STAGE[0] = "load_x"
            if ti == 0:
                for c_ in range(NXS):
                    load_x_dma(0, c_)
                for tcg in range(4):
                    load_x_chunk(0, tcg, norm1_first)
            for tcg in range(4, 8):
                load_x_chunk(ti, tcg, norm1_first)
            xn_flush(0)
            xn_flush(1)
            for l in range(L):
                base = l * LC
                first_piece(("w_in", l, 0), lambda s, slot, slb, bw: mixer_p(s, l, slot, slb, first=(ti == 0 and s == 0), between=bw))
                pooling(0, l, first=(ti == 0))
                pooling(1, l, first=False, POOL=DVE, tv=ptv2, tb=ptb2, queue=True)
                piece_step(("w_in", l, 2), lambda s, slot, slb: mixer_v(s, l, slot, slb))
                piece_step(("w_in", l, 1), lambda s, slot, slb: mixer_u(s, l, slot, slb))
                pool_drain()
                STAGE[0] = "mix"
                for s in range(NS):
                    mixer_mix(s, l)
                piece_step(("w_out", l, 0), lambda s, slot, slb: proj_resid(s, 0, slot, slb))
                norm_piece(("w_out", l, 1), lambda s, slot, slb, bw: proj_resid(s, 1, slot, slb, between=bw), base + O_GX)
                first_piece(("wq", l, 0), lambda s, slot, slb, bw: proj_q(s, 0, slot, slb, between=bw))
                piece_step(("wq", l, 1), lambda s, slot, slb: proj_q(s, 1, slot, slb))
                if ti == 0:
                    STAGE[0] = "kv"
                    kv_compute(l)
                STAGE[0] = "attn"
                attention_all(l)
                piece_step(("wo", l, 0), lambda s, slot, slb: proj_resid(s, 0, slot, slb))
                norm_piece(("wo", l, 1), lambda s, slot, slb, bw: proj_resid(s, 1, slot, slb, between=bw), base + O_GFFN)
                first_piece(("w_up", l, 0), lambda s, slot, slb, bw: ffn_up(s, l, 0, slot, slb, between=bw))
                for i in range(1, 11):
                    piece_step(("w_up", l, i), lambda s, slot, slb: ffn_up(s, l, i, slot, slb))
                ffn_flush()
                STAGE[0] = "w_down"
                sl0 = ws_acquire(pidx[("w_down", l, 0)])
                sl1 = ws_acquire(pidx[("w_down", l, 1)], ahead=1)
                for (k0, k1) in ((0, 16), (16, 22)):
                    for m, (slot, slb) in ((0, sl0), (1, sl1)):
                        for s in range(NS):
                            ffn_down(s, m, slot, slb, k0=k0, k1=k1)
                ws_release()
                ws_release()
                for m in range(2, 7):
                    piece_step(("w_down", l, m), lambda s, slot, slb: ffn_down(s, m, slot, slb))
                if l + 1 < L:
                    norm_piece(("w_down", l, 7), lambda s, slot, slb, bw: ffn_down(s, 7, slot, slb, pre_evac=(None if bw is None else xn_flush_0)),
                               (l + 1) * LC + O_GMIX)
                else:
                    if ti + 1 < NT:
                        for c_ in range(NXS):
                            load_x_dma(ti + 1, c_)
                    piece_step(("w_down", l, 7), lambda s, slot, slb: ffn_down(s, 7, slot, slb), after=final_norm)
            STAGE[0] = "final"
            for tc in range(4):
                final_store_chunk(0, ti, tc)
            for tc in range(4):
                final_store_chunk(1, ti, tc)
                if ti + 1 < NT:
                    load_x_chunk(ti + 1, tc, norm1_first)

        for i in range(4):
            if stg_st[i].val:
                ACT.ops.append(lambda e, i=i: e.wait_ge(stg_st[i].sem, stg_st[i].val))

        with nc.Block() as block:
            @block.tensor
            def _(e):
                for op in PE.ops:
                    op(e)

            @block.scalar
            def _(e):
                for op in ACT.ops:
                    op(e)

            @block.vector
            def _(e):
                for op in DVE.ops:
                    op(e)

            @block.gpsimd
            def _(e):
                for op in POOL.ops:
                    op(e)

            @block.sync
            def _(e):
                for op in SP.ops:
                    op(e)
    return nc


def make_consts(L, norm_mix_g, norm_xattn_g, norm_ffn_g, mem_norm_g, pool_scale, sgu_g, conv_w, conv_b, sgu_b, final_norm_g):
    NC_ = L * LC + 8 + 64 + 1
    c = np.zeros((128, NC_), np.float32)
    col = lambda v: np.asarray(v, np.float32).reshape(-1, 128).T
    for l in range(L):
        b = l * LC
        c[:, b + O_GMIX:b + O_GMIX + 8] = col(norm_mix_g[l])
        c[:, b + O_GX:b + O_GX + 8] = col(norm_xattn_g[l])
        c[:, b + O_GFFN:b + O_GFFN + 8] = col(norm_ffn_g[l])
        c[:, b + O_GMEM:b + O_GMEM + 8] = col(mem_norm_g[l])
        c[:, b + O_PSC:b + O_PSC + 4] = col(pool_scale[l])
        c[:, b + O_SGG:b + O_SGG + 4] = col(sgu_g[l])
        pair = lambda v: col(v).reshape(128, 2, 22).transpose(0, 2, 1).reshape(128, 44)
        c[:, b + O_CW0:b + O_CW0 + 44] = pair(conv_w[l, 0])
        c[:, b + O_CW1:b + O_CW1 + 44] = pair(conv_w[l, 1])
        c[:, b + O_CW2:b + O_CW2 + 44] = pair(conv_w[l, 2])
        c[:, b + O_CB:b + O_CB + 44] = pair(conv_b[l])
        c[:, b + O_BBC:b + O_BBC + 512] = np.broadcast_to(np.asarray(sgu_b[l], np.float32).reshape(1, 512), (128, 512))
    c[:, L * LC:L * LC + 8] = col(final_norm_g)
    inv = np.zeros((4, 16), np.float32)
    for g, win in enumerate(WINS):
        inv[g] = float(win) / np.minimum(np.arange(16) + 1, win)
    c[:, L * LC + 8:L * LC + 72] = np.broadcast_to(inv.reshape(1, 64), (128, 64))
    c[:, L * LC + 72] = EPS
    return c


_cache = {}


def run(inputs, S, L, n_cores):
    f = lambda a: np.ascontiguousarray(np.asarray(a, dtype=np.float32))
    key = (S, L)
    if key not in _cache:
        _cache[key] = build(S, L)
    nc = _cache[key]
    consts = make_consts(L, *[np.asarray(inputs[k], np.float32) for k in
                              ("norm_mix_g", "norm_xattn_g", "norm_ffn_g", "mem_norm_g", "pool_scale", "sgu_g", "conv_w", "conv_b", "sgu_b", "final_norm_g")])
    ident = np.eye(128, dtype=np.float32)
    tril = np.triu(np.ones((128, 128), np.float32))
    shared = {k: f(inputs[k]) for k in ("w_in", "pool_w", "sgu_w", "w_out", "wq", "wk", "wv", "wo", "w_up", "w_down")}
    shared.update(consts=consts, ident=ident, trilmask=tril)
    x = f(inputs["x"])
    mem = f(inputs["mem"])
    in_maps = []
    for c in range(n_cores):
        m = dict(shared)
        m["x"] = x[c]
        m["mem"] = mem[c]
        in_maps.append(m)
    res = run_bass_kernel_spmd(nc, in_maps, core_ids=list(range(n_cores)))
    return np.stack([np.asarray(r["out"], dtype=np.float32) for r in res.results], axis=0)


def kernel(**inputs):
    return run(inputs, 8192, 2, 8)
```
